# Optimizing a Trainium2 kernel written in Bass

```python
import jax, jax.numpy as jnp
from jax import lax
import numpy as np

D_MODEL = 1024
BATCH = 2
SEQ = 16384
DEPTH = 1
DEC_BATCH = 16
DEC_SEQ = 64
PAST_LEN = 4096

CHUNK = 64
MIX_WIDTH = D_MODEL
CONV_CH = MIX_WIDTH // 2
POOL_CH = MIX_WIDTH - CONV_CH
CONV_WIDTH = 31
CONV_HIST = CONV_WIDTH - 1
POOL_WINDOWS = (2, 4, 8, 16)
N_POOL_GROUPS = len(POOL_WINDOWS)
POOL_GROUP = POOL_CH // N_POOL_GROUPS
POOL_HIST = max(POOL_WINDOWS) - 1
IN_COLS = 2 * CONV_CH + POOL_CH
D_FF = ((8 * D_MODEL // 3 + 127) // 128) * 128
N_MOD = 9
ALPHA = (2.0 * DEPTH) ** 0.25
BETA = (8.0 * DEPTH) ** -0.25
LN_EPS = 1e-5

kernel_name = 'hybrid_conv_pool_streaming_encoder_step'


def _layernorm(x, g, b):
    xf = x.astype(jnp.float32)
    mu = jnp.mean(xf, axis=-1, keepdims=True)
    var = jnp.mean(jnp.square(xf - mu), axis=-1, keepdims=True)
    return ((xf - mu) * lax.rsqrt(var + LN_EPS)).astype(x.dtype) * g + b


def _swiglu(u, w_in, w_down):
    g, v = jnp.split(u @ w_in, 2, axis=-1)
    return (jax.nn.silu(g) * v) @ w_down


def _token_mixer(u, conv_hist, pool_hist, pos, w_in, b_in, conv_w, conv_b, cln_g, cln_b,
                 pool_w, pool_b, pool_scale, w_out, b_out):
    B, T, _ = u.shape
    z = u @ w_in + b_in
    za = z[..., :CONV_CH]
    zb = z[..., CONV_CH:2 * CONV_CH]
    zp = z[..., 2 * CONV_CH:]
    glu = za * jax.nn.sigmoid(zb)
    cin = jnp.concatenate([conv_hist.astype(glu.dtype), glu], axis=1)
    conv = lax.conv_general_dilated(cin, conv_w[:, None, :].astype(cin.dtype), window_strides=(1,),
                                    padding='VALID', dimension_numbers=('NWC', 'WIO', 'NWC'),
                                    feature_group_count=CONV_CH) + conv_b
    a = jax.nn.silu(_layernorm(conv, cln_g, cln_b))
    pin = jnp.concatenate([pool_hist.astype(zp.dtype), zp], axis=1)
    cs = jnp.pad(jnp.cumsum(pin.astype(jnp.float32), axis=1), ((0, 0), (1, 0), (0, 0)))
    end = cs[:, POOL_HIST + 1:]
    zp_f = zp.astype(jnp.float32)
    groups = []
    for g, w in enumerate(POOL_WINDOWS):
        sl = slice(g * POOL_GROUP, (g + 1) * POOL_GROUP)
        wsum = end[..., sl] - cs[:, POOL_HIST + 1 - w:POOL_HIST + 1 - w + T, sl]
        cnt = jnp.minimum(pos + 1, w).astype(jnp.float32)[None, :, None]
        groups.append(wsum / cnt - zp_f[..., sl])
    pooled = jnp.stack(groups, axis=2).astype(u.dtype)
    pm = jnp.einsum('btgc,gcd->btgd', pooled, pool_w) + pool_b
    pm = pm.reshape(B, T, POOL_CH) * pool_scale
    y = jnp.concatenate([a, pm], axis=-1) @ w_out + b_out
    return y, cin[:, -CONV_HIST:], pin[:, -POOL_HIST:]


def _trunk(x, c, conv_cache, pool_cache, pos0, ada_w, ada_b, ln_g, ln_b, ffn_w_in, ffn_w_down,
           mix_w_in, mix_b_in, conv_w, conv_b, conv_ln_g, conv_ln_b, pool_w, pool_b, pool_scale,
           mix_w_out, mix_b_out):
    B, T, _ = x.shape
    pos = pos0 + jnp.arange(T)
    new_conv, new_pool = [], []
    for l in range(DEPTH):
        mod = (jax.nn.silu(c) @ ada_w[l] + ada_b[l]).reshape(B, N_MOD, 1, D_MODEL)
        shift = lambda k: mod[:, 3 * k]
        scale = lambda k: mod[:, 3 * k + 1]
        gate = lambda k: mod[:, 3 * k + 2]
        u = x * (1 + scale(0)) + shift(0)
        x = _layernorm(ALPHA * x + 0.5 * gate(0) * _swiglu(u, ffn_w_in[l, 0], ffn_w_down[l, 0]),
                       ln_g[l, 0], ln_b[l, 0])
        u = x * (1 + scale(1)) + shift(1)
        m, hc, hp = _token_mixer(u, conv_cache[l], pool_cache[l], pos, mix_w_in[l], mix_b_in[l],
                                 conv_w[l], conv_b[l], conv_ln_g[l], conv_ln_b[l], pool_w[l],
                                 pool_b[l], pool_scale[l], mix_w_out[l], mix_b_out[l])
        x = _layernorm(ALPHA * x + gate(1) * m, ln_g[l, 1], ln_b[l, 1])
        u = x * (1 + scale(2)) + shift(2)
        x = _layernorm(ALPHA * x + 0.5 * gate(2) * _swiglu(u, ffn_w_in[l, 1], ffn_w_down[l, 1]),
                       ln_g[l, 2], ln_b[l, 2])
        new_conv.append(hc)
        new_pool.append(hp)
    return x, jnp.stack(new_conv, axis=0), jnp.stack(new_pool, axis=0)


def setup_inputs(seed: int = 0) -> dict:
    key = jax.random.key(seed)
    ks = jax.random.split(key, 24)
    f32 = jnp.float32
    def n(k, shape, s):
        return jax.random.normal(k, shape, f32) * s
    return {
        'x_prompt': n(ks[0], (BATCH, SEQ, D_MODEL), 1.0),
        'x_sample': n(ks[1], (DEC_BATCH, DEC_SEQ, D_MODEL), 1.0),
        'cache_conv': n(ks[2], (DEPTH, DEC_BATCH, CONV_HIST, CONV_CH), 0.5),
        'cache_pool': n(ks[3], (DEPTH, DEC_BATCH, POOL_HIST, POOL_CH), 1.0),
        'c_prompt': n(ks[4], (BATCH, D_MODEL), 1.0),
        'c_sample': n(ks[5], (DEC_BATCH, D_MODEL), 1.0),
        'ada_w': n(ks[6], (DEPTH, D_MODEL, N_MOD * D_MODEL), D_MODEL ** -0.5),
        'ada_b': n(ks[7], (DEPTH, N_MOD * D_MODEL), 0.02),
        'ln_g': 1.0 + n(ks[8], (DEPTH, 3, D_MODEL), 0.05),
        'ln_b': n(ks[9], (DEPTH, 3, D_MODEL), 0.02),
        'ffn_w_in': n(ks[10], (DEPTH, 2, D_MODEL, 2 * D_FF), D_MODEL ** -0.5),
        'ffn_w_down': n(ks[11], (DEPTH, 2, D_FF, D_MODEL), BETA * D_FF ** -0.5),
        'mix_w_in': n(ks[12], (DEPTH, D_MODEL, IN_COLS), D_MODEL ** -0.5),
        'mix_b_in': n(ks[13], (DEPTH, IN_COLS), 0.02),
        'conv_w': n(ks[14], (DEPTH, CONV_WIDTH, CONV_CH), CONV_WIDTH ** -0.5),
        'conv_b': n(ks[15], (DEPTH, CONV_CH), 0.02),
        'conv_ln_g': 1.0 + n(ks[16], (DEPTH, CONV_CH), 0.05),
        'conv_ln_b': n(ks[17], (DEPTH, CONV_CH), 0.02),
        'pool_w': n(ks[18], (DEPTH, N_POOL_GROUPS, POOL_GROUP, POOL_GROUP), POOL_GROUP ** -0.5),
        'pool_b': n(ks[19], (DEPTH, N_POOL_GROUPS, POOL_GROUP), 0.02),
        'pool_scale': 1.0 + n(ks[20], (DEPTH, POOL_CH), 0.1),
        'mix_w_out': n(ks[21], (DEPTH, MIX_WIDTH, D_MODEL), BETA * MIX_WIDTH ** -0.5),
        'mix_b_out': n(ks[22], (DEPTH, D_MODEL), 0.02),
    }


def reference(x_prompt, x_sample, cache_conv, cache_pool, c_prompt, c_sample, ada_w, ada_b, ln_g,
              ln_b, ffn_w_in, ffn_w_down, mix_w_in, mix_b_in, conv_w, conv_b, conv_ln_g, conv_ln_b,
              pool_w, pool_b, pool_scale, mix_w_out, mix_b_out):
    zero_conv = jnp.zeros((DEPTH, x_prompt.shape[0], CONV_HIST, CONV_CH), x_prompt.dtype)
    zero_pool = jnp.zeros((DEPTH, x_prompt.shape[0], POOL_HIST, POOL_CH), x_prompt.dtype)
    y_prompt, state_conv_prompt, state_pool_prompt = _trunk(
        x_prompt, c_prompt, zero_conv, zero_pool, 0, ada_w, ada_b, ln_g, ln_b, ffn_w_in, ffn_w_down,
        mix_w_in, mix_b_in, conv_w, conv_b, conv_ln_g, conv_ln_b, pool_w, pool_b, pool_scale,
        mix_w_out, mix_b_out)
    y_sample, state_conv_sample, state_pool_sample = _trunk(
        x_sample, c_sample, cache_conv, cache_pool, PAST_LEN, ada_w, ada_b, ln_g, ln_b, ffn_w_in,
        ffn_w_down, mix_w_in, mix_b_in, conv_w, conv_b, conv_ln_g, conv_ln_b, pool_w, pool_b,
        pool_scale, mix_w_out, mix_b_out)
    return (y_prompt, y_sample, state_conv_prompt, state_pool_prompt, state_conv_sample, state_pool_sample)
```

```python
import contextlib
import numpy as np
import concourse.bass as bass
import concourse.mybir as mybir
from concourse.bass_utils import run_bass_kernel_spmd

F32 = mybir.dt.float32
BF16 = mybir.dt.bfloat16
F32R = mybir.dt.float32r
AF = mybir.ActivationFunctionType
ALU = mybir.AluOpType

D = 1024
DFF = 2816
NFT = 8
NKF = 22
CW = 31
CH = 30
PH = 15
NCORES = 8
PCH = 4096
ALPHA = 2.0 ** 0.25
EPS = 1e-5
NSF = 3
NSM = 2
SLOT_ELEMS = 4096
NSEQ = 4

_off = {}
_o = 0
for _n, _w in [("ident", 128), ("ada_b", 72), ("ln_g", 24), ("ln_b", 24), ("mixb", 12),
               ("conv_w", 124), ("conv_b", 4), ("cln_g", 4), ("cln_b", 4), ("pool_b", 4),
               ("pool_s", 4), ("b_out", 8), ("mask", 1), ("invcnt", 64), ("eps", 2)]:
    _off[_n] = _o
    _o += _w
NV = _o

MIX_CT = [4, 0, 5, 1, 6, 2, 7, 3, 8, 9, 10, 11]


class _Op:
    __slots__ = ("eng", "fn", "deps", "pos", "signal", "count", "dma_key", "dma_val", "is_dma")


class Sched:
    ENG = ("pe", "act", "dve", "pool", "sp")

    def __init__(self):
        self.ops = []
        self.lastw = {}
        self.readers = {}
        self.eng_ops = {e: [] for e in self.ENG}
        self.dma_counts = {}

    def add(self, eng, fn, reads=(), writes=(), dma=None):
        op = _Op()
        op.eng = eng
        op.fn = fn
        op.is_dma = dma is not None
        op.signal = False
        op.count = 0
        deps = set()
        writes = list(writes) + [k for k in reads if k[0] == "ps"]
        reads = [k for k in reads if k[0] != "ps"]
        for k in reads:
            w = self.lastw.get(k)
            if w is not None:
                deps.add(w)
        for k in writes:
            w = self.lastw.get(k)
            if w is not None:
                deps.add(w)
            for r in self.readers.get(k, ()):
                deps.add(r)
        for k in reads:
            self.readers.setdefault(k, []).append(op)
        for k in writes:
            self.lastw[k] = op
            self.readers[k] = []
        op.deps = deps
        op.pos = len(self.eng_ops[eng])
        self.eng_ops[eng].append(op)
        if dma is not None:
            n = self.dma_counts.get(dma, 0) + 1
            self.dma_counts[dma] = n
            op.dma_key = dma
            op.dma_val = 16 * n
        else:
            op.dma_key = None
            op.dma_val = 0
        self.ops.append(op)
        return op

    @staticmethod
    def _needs(op, d):
        if d.is_dma:
            return True
        if d.eng != op.eng:
            return True
        if op.is_dma:
            return True
        if op.eng == "pe":
            return False
        return (op.pos - d.pos) <= 2

    def finalize(self):
        for op in self.ops:
            for d in op.deps:
                if not d.is_dma and self._needs(op, d):
                    d.signal = True
        for e in self.ENG:
            c = 0
            for op in self.eng_ops[e]:
                if op.signal:
                    c += 1
                op.count = c

    def emit_engine(self, e, eng, eng_sem, dma_sem, final_waits=()):
        waited = {}
        for op in self.eng_ops[e]:
            waits = {}
            for d in op.deps:
                if not self._needs(op, d):
                    continue
                if d.is_dma:
                    key = ("dma", d.dma_key)
                    val = d.dma_val
                else:
                    key = ("eng", d.eng)
                    val = d.count
                if val > waits.get(key, 0):
                    waits[key] = val
            for key, val in waits.items():
                if waited.get(key, 0) >= val:
                    continue
                waited[key] = val
                sem = dma_sem[key[1]] if key[0] == "dma" else eng_sem[key[1]]
                eng.wait_ge(sem, val)
            ins = op.fn(eng)
            if op.is_dma:
                ins.then_inc(dma_sem[op.dma_key], 16)
            elif op.signal:
                ins.then_inc(eng_sem[e], 1)
        for key in final_waits:
            eng.wait_ge(dma_sem[key], 16 * self.dma_counts[key])


class TileCfg:
    def __init__(self, name, T, nseg, L, segseq, blocks, xsrc, outblocks, kind, tidx):
        self.name = name
        self.T = T
        self.nseg = nseg
        self.L = L
        self.segseq = segseq
        self.blocks = blocks
        self.xsrc = xsrc
        self.outblocks = outblocks
        self.kind = kind
        self.tidx = tidx
        self.par = 0


def build_program(n_ptiles=8, do_s=True, stop_after=None, interleave=True):
    nc = bass.Bass("TRN2", target_bir_lowering=False)

    def din(name, shape):
        return nc.dram_tensor(name, shape, F32, kind="ExternalInput").ap()

    def dout(name, shape):
        return nc.dram_tensor(name, shape, F32, kind="ExternalOutput").ap()

    xs_d = din("xs", [192, D])
    xp_d = din("xp", [PCH, D])
    cT_d = din("cT", [128, NFT * NSEQ])
    cc_d = din("ccache", [128, 4 * 2 * CH])
    pc_d = din("pcache", [128, 4 * 2 * PH])
    vecs_d = din("vecs", [128, NV])
    ada_d = din("ada_w", [36, 128, 2048])
    w1_d = din("w1", [2, 11, 128, 4096])
    wd_d = din("wd", [2, 8, 128, DFF])
    wmi_d = din("wmi", [3, 128, 4096])
    wpo_d = din("wpo", [128, 512])
    wo_d = din("wo", [2, 128, 4096])

    yp_d = dout("yp", [PCH, D])
    ys_d = dout("ys", [192, D])
    scp_d = dout("scp", [CH, 512])
    spp_d = dout("spp", [PH, 512])
    scs_d = dout("scs", [2, CH, 512])
    sps_d = dout("sps", [2, PH, 512])

    S = Sched()
    dry = [False]

    class _Dummy:
        def add(self, *a, **k):
            return None
    _dummy = _Dummy()

    def ADD(*a, **k):
        return (_dummy if dry[0] else S).add(*a, **k)
    es = contextlib.ExitStack()

    def sb(name, shape, dt=F32):
        return es.enter_context(nc.sbuf_tensor("s_" + name, shape, dt))

    with es:
        vecs = sb("vecs", [128, NV])
        ones = sb("ones", [128, 128], BF16)
        cT = sb("cT", [128, NFT, NSEQ])
        sc = sb("sc", [128, NFT, NSEQ], BF16)
        modsb = sb("modsb", [128, 72, NSEQ])
        par = sb("par", [128, 264])
        xt = sb("xt", [128, 4, D])
        yt = sb("yt", [128, 2, D])
        xfm2 = [sb("xfm0", [128, NFT, 512]), sb("xfm1", [128, NFT, 512])]
        u = sb("u", [128, NFT, 512], BF16)
        h = sb("h", [128, NKF, 512], BF16)
        sg = sb("sg", [128, 2, 512])
        sq = sb("sq", [128, 2, 512], BF16)
        rr = sb("rr", [128, 2, 512], BF16)
        mean = sb("mean", [128, 512])
        rstd = sb("rstd", [128, 512])
        cin = sb("cin", [128, 4, 544])
        pin = sb("pin", [128, 4, 528])
        acc = sb("acc", [128, 4, 512])
        sig = sb("sig", [128, 2, 512])
        act = sb("act", [128, 8, 512], BF16)
        pooled = sb("pooled", [128, 4, 512], BF16)
        sA = sb("sA", [128, 528])
        sB = sb("sB", [128, 528])
        t16 = sb("t16", [128, 16])
        u2 = sb("u2", [128, NFT, 512], BF16)
        ringF = sb("ringF", [128, NSF, SLOT_ELEMS], BF16)
        ringM = sb("ringM", [128, NSM, SLOT_ELEMS], BF16)
        ps = [es.enter_context(nc.psum_tensor(f"ps{i}", [128, 512], F32)) for i in range(8)]

        ident = vecs[:, _off["ident"]:_off["ident"] + 128]

        def vcol(name, idx):
            o = _off[name] + idx
            return vecs[:, o:o + 1]

        pcol = [0]

        def palloc(n):
            o = pcol[0]
            pcol[0] += n
            return o

        P_s1 = [[palloc(8) for s in range(3)] for k in range(3)]
        P_gp = [[palloc(8) for s in range(3)] for k in range(3)]
        P_tB = [[palloc(8) for s in range(3)] for k in range(3)]
        P_xb0 = [palloc(8) for s in range(3)]
        P_tmp = palloc(8)
        P_hb = palloc(12)

        def pc(o, ft):
            return par[:, o + ft:o + ft + 1]

        def modv(j, s):
            return modsb[:, j * 8:(j + 1) * 8, s]

        def modc(j, ft, s):
            return modsb[:, j * 8 + ft, s:s + 1]

        ADD("sp", lambda e: e.dma_start(out=vecs[:], in_=vecs_d), writes=[("vecs",)], dma="d_vecs")
        ADD("sp", lambda e: e.dma_start(out=cT[:].rearrange("p k s -> p (k s)"), in_=cT_d),
              writes=[("cT",)], dma="d_cT")
        ADD("act", lambda e: e.activation(out=ones[:], in_=ident, func=AF.Identity, scale=0.0, bias=1.0),
              reads=[("vecs",)], writes=[("ones",)])
        ADD("dve", lambda e: e.tensor_scalar(out=par[:, P_hb:P_hb + 12],
                                             in0=vecs[:, _off["mixb"]:_off["mixb"] + 12],
                                             scalar1=0.5, scalar2=None, op0=ALU.mult),
            reads=[("vecs",)], writes=[("par", "hb")])
        ADD("act", lambda e: e.activation(out=sc[:], in_=cT[:], func=AF.Silu),
              reads=[("cT",)], writes=[("sc",)])


        ada_issued = [0]

        def ada_stage(ci):
            slot = ci % 4
            sbuf_, skn = (act, "act") if slot < 2 else (u2, "u2")
            h0 = (slot % 2) * 4
            stage = sbuf_[:, h0:h0 + 4, :].rearrange("p a (k n) -> p (a k) n", k=2)
            skeys = [(skn, h0 + i_) for i_ in range(4)]
            return slot, stage, skeys

        def ada_dma_upto(cmax):
            if dry[0]:
                return
            while ada_issued[0] < min(cmax, 36):
                cj = ada_issued[0]
                slot_, stage_, skeys_ = ada_stage(cj)

                def _ld(e, stage_=stage_, cj=cj):
                    return e.dma_start(out=stage_.rearrange("p k n -> p (k n)"), in_=ada_d[cj])
                ADD("pool", _ld, writes=skeys_, dma=("ada", slot_))
                ada_issued[0] += 1

        def g_ada(c0, c1):
          for ci in range(c0, c1):
              ada_dma_upto(ci + 4)
              slot, stage, skeys = ada_stage(ci)
              for n in range(2):
                  tile_i = ci * 2 + n
                  j = tile_i // 8
                  bank = 6 + (j % 2)

                  def _mm(e, stage=stage, n=n, tile_i=tile_i, bank=bank):
                      ins = None
                      for kt in range(NFT):
                          ins = e.matmul(out=ps[bank][:, tile_i * NSEQ:(tile_i + 1) * NSEQ],
                                         lhsT=stage[:, kt, n * 128:(n + 1) * 128],
                                         rhs=sc[:, kt, :], start=(kt == 0), stop=(kt == NFT - 1))
                      return ins
                  ADD("pe", _mm, reads=skeys + [("sc",)],
                        writes=[("ps", bank)])
                  if tile_i % 8 == 7:
                      for s in range(NSEQ):
                          def _ev(e, j=j, s=s, bank=bank):
                              pv = ps[bank][:, j * 8 * NSEQ:(j + 1) * 8 * NSEQ].rearrange(
                                  "p (t s) -> p t s", s=NSEQ)[:, :, s]
                              return e.tensor_tensor(out=modv(j, s), in0=pv,
                                                     in1=vecs[:, _off["ada_b"] + j * 8:_off["ada_b"] + j * 8 + 8],
                                                     op=ALU.add)
                          ADD("dve", _ev, reads=[("ps", bank), ("vecs",)], writes=[("mod", j)])
                      k = j // 3
                      if j % 3 == 2:
                          coef = (1.0 if k == 1 else 0.5) / ALPHA
                          for s in range(3):
                              ADD("dve", lambda e, k=k, s=s: e.tensor_scalar(
                                  out=par[:, P_s1[k][s]:P_s1[k][s] + 8], in0=modv(3 * k + 1, s),
                                  scalar1=1.0, scalar2=None, op0=ALU.add),
                                  reads=[("mod", 3 * k + 1)], writes=[("par", k)])
                              ADD("dve", lambda e, k=k, s=s, coef=coef: e.tensor_scalar(
                                  out=par[:, P_gp[k][s]:P_gp[k][s] + 8], in0=modv(3 * k + 2, s),
                                  scalar1=coef, scalar2=None, op0=ALU.mult),
                                  reads=[("mod", 3 * k + 2)], writes=[("par", k)])
                              if k >= 1:
                                  lb = _off["ln_b"] + (k - 1) * 8
                                  ADD("dve", lambda e, k=k, s=s, lb=lb: e.tensor_tensor(
                                      out=par[:, P_tmp:P_tmp + 8], in0=vecs[:, lb:lb + 8],
                                      in1=par[:, P_s1[k][s]:P_s1[k][s] + 8], op=ALU.mult),
                                      reads=[("par", k), ("vecs",)], writes=[("ptmp",)])
                                  ADD("dve", lambda e, k=k, s=s: e.tensor_tensor(
                                      out=par[:, P_tB[k][s]:P_tB[k][s] + 8], in0=par[:, P_tmp:P_tmp + 8],
                                      in1=modv(3 * k, s), op=ALU.add),
                                      reads=[("ptmp",), ("mod", 3 * k)], writes=[("par", k)])
                              if k == 1:
                                  bo = _off["b_out"]
                                  lb0 = _off["ln_b"]
                                  ADD("dve", lambda e, s=s, bo=bo: e.tensor_tensor(
                                      out=par[:, P_tmp:P_tmp + 8], in0=par[:, P_gp[1][s]:P_gp[1][s] + 8],
                                      in1=vecs[:, bo:bo + 8], op=ALU.mult),
                                      reads=[("par", 1), ("vecs",)], writes=[("ptmp",)])
                                  ADD("dve", lambda e, s=s, lb0=lb0: e.tensor_tensor(
                                      out=par[:, P_xb0[s]:P_xb0[s] + 8], in0=par[:, P_tmp:P_tmp + 8],
                                      in1=vecs[:, lb0:lb0 + 8], op=ALU.add),
                                      reads=[("ptmp",), ("vecs",)], writes=[("par", 1)])
              yield 7.0

        def f_chunks(f):
            return [(w1_d[f, c], 4096) for c in range(11)] + [(wd_d[f, mo], DFF) for mo in range(8)]
        m_chunks = [(wmi_d[c], 4096) for c in range(3)] + [(wpo_d, 512)] + [(wo_d[c], 4096) for c in range(2)]
        NFC = 19
        NMC = 6
        f_order = []
        f_index = {}
        wst = {"F": 0, "M": 0, "ntiles": 0}

        def _prefetch(which, upto):
            if which == "F":
                total = len(f_order) * NFC
            else:
                total = wst["ntiles"] * NMC
            upto = min(upto, total - 1)
            while wst[which] <= upto:
                gi = wst[which]
                if which == "F":
                    tk = f_order[gi // NFC]
                    src, n = f_chunks(0 if tk[1] == 0 else 1)[gi % NFC]
                    slot = gi % NSF
                    ringt = ringF
                else:
                    src, n = m_chunks[gi % NMC]
                    slot = gi % NSM
                    ringt = ringM

                def _ld(e, src=src, n=n, slot=slot, ringt=ringt):
                    return e.dma_start(out=ringt[:, slot, 0:n], in_=src)
                ADD("pool", _ld, writes=[("ring" + which, slot)], dma=("ring" + which, slot))
                wst[which] += 1

        def wslotF(tidx, k, local):
            if dry[0]:
                return 0
            gi = f_index[(tidx, k)] * NFC + local
            _prefetch("F", gi + NSF - 1)
            return gi % NSF

        def wslotM(tidx, local):
            if dry[0]:
                return 0
            gi = tidx * NMC + local
            _prefetch("M", gi + NSM - 1)
            return gi % NSM

        CH_WMI = 0
        CH_WPO = 3
        CH_WO = 4

        def segs(cfg):
            return [(si, cfg.segseq[si], si * cfg.L, (si + 1) * cfg.L) for si in range(cfg.nseg)]

        x_issued = set()

        def issue_x(cfg):
            if dry[0] or cfg.tidx in x_issued:
                return
            x_issued.add(cfg.tidx)
            for bi, (c0, rows) in enumerate(cfg.blocks):
                def _ld(e, bi=bi, c0=c0, rows=rows, cfg=cfg):
                    return e.dma_start(out=xt[:rows, bi, :], in_=cfg.xsrc[c0:c0 + rows, :])
                ADD("sp", _ld, writes=[("xt", bi)], dma=("xin", bi))

        def load_in(cfg):
            T = cfg.T
            xfm = xfm2[cfg.par]
            XK = lambda ft_: ("xfm", cfg.par, ft_)
            issue_x(cfg)
            for ft in range(NFT):
                bank = 4 + (ft % 2)

                def _tr(e, ft=ft, bank=bank):
                    ins = None
                    for bi, (c0, rows) in enumerate(cfg.blocks):
                        ins = e.transpose(out=ps[bank][:, c0:c0 + rows],
                                          in_=xt[:rows, bi, ft * 128:(ft + 1) * 128],
                                          identity=ident[:rows, :rows])
                    return ins
                ADD("pe", _tr, reads=[("xt", bi) for bi in range(len(cfg.blocks))] + [("vecs",)],
                      writes=[("ps", bank)])
                ADD("act", lambda e, ft=ft, bank=bank: e.activation(out=xfm[:, ft, 0:T], in_=ps[bank][:, 0:T],
                                                                   func=AF.Copy),
                      reads=[("ps", bank)], writes=[XK(ft)])
                for (si, s, a, b) in segs(cfg):
                    ADD("act", lambda e, ft=ft, bank=bank, s=s, a=a, b=b: e.activation(
                        out=u[:, ft, a:b], in_=ps[bank][:, a:b], func=AF.Identity,
                        scale=pc(P_s1[0][s], ft), bias=modc(0, ft, s)),
                        reads=[("ps", bank), ("par", 0), ("mod", 0)], writes=[("u", ft)])
                yield 1.0 * T / 512
            if not dry[0] and cfg.tidx + 1 < len(tiles):
                issue_x(tiles[cfg.tidx + 1])

        def layer_norm(cfg, k):
            T = cfg.T
            xfm = xfm2[cfg.par]
            XK = lambda ft_: ("xfm", cfg.par, ft_)
            ubuf, ukn = (act, "act") if k == 0 else (u2, "u2")
            if k == 1:
                yield ("need", ("f2in", cfg.tidx - 1))
            yield ("lock", "ln")
            for ft in range(NFT):
                r = ft % 2
                ADD("dve", lambda e, ft=ft, r=r: e.tensor_copy(out=rr[:, r, 0:T], in_=xfm[:, ft, 0:T]),
                      reads=[XK(ft)], writes=[("rr", r)])
                ADD("act", lambda e, ft=ft, r=r: e.activation(out=sq[:, r, 0:T], in_=xfm[:, ft, 0:T], func=AF.Square),
                      reads=[XK(ft)], writes=[("sq", r)])

                def _st(e, ft=ft, r=r):
                    e.matmul(out=ps[6][:, 0:T], lhsT=ones[:], rhs=rr[:, r, 0:T],
                             start=(ft == 0), stop=(ft == NFT - 1))
                    return e.matmul(out=ps[7][:, 0:T], lhsT=ones[:], rhs=sq[:, r, 0:T],
                                    start=(ft == 0), stop=(ft == NFT - 1))
                ADD("pe", _st, reads=[("rr", r), ("sq", r), ("ones",)], writes=[("ps", 6), ("ps", 7)])
                yield 0.6 * T / 512
            stats(cfg, 1.0 / D, 0)
            yield 6.0
            def opA(ft):
                ADD("dve", lambda e, ft=ft: e.tensor_tensor(out=xfm[:, ft, 0:T], in0=xfm[:, ft, 0:T],
                                                               in1=mean[:, 0:T], op=ALU.subtract),
                      reads=[XK(ft), ("mean",)], writes=[XK(ft)])

            def opB(ft):
                ADD("dve", lambda e, ft=ft: e.scalar_tensor_tensor(
                    out=xfm[:, ft, 0:T], in0=xfm[:, ft, 0:T], scalar=vcol("ln_g", k * 8 + ft),
                    in1=rstd[:, 0:T], op0=ALU.mult, op1=ALU.mult),
                    reads=[XK(ft), ("rstd",), ("vecs",)], writes=[XK(ft)])
                if k < 2:
                    for (si, s, a, b) in segs(cfg):
                        ADD("act", lambda e, ft=ft, s=s, a=a, b=b: e.activation(
                            out=ubuf[:, ft, a:b], in_=xfm[:, ft, a:b], func=AF.Identity,
                            scale=pc(P_s1[k + 1][s], ft), bias=pc(P_tB[k + 1][s], ft)),
                            reads=[XK(ft), ("par", k + 1)], writes=[(ukn, ft)])

            def opX(ft):
                if k == 0:
                    for (si, s, a, b) in segs(cfg):
                        ADD("dve", lambda e, ft=ft, s=s, a=a, b=b: e.tensor_scalar(
                            out=xfm[:, ft, a:b], in0=xfm[:, ft, a:b], scalar1=pc(P_xb0[s], ft),
                            scalar2=None, op0=ALU.add),
                            reads=[XK(ft), ("par", 1)], writes=[XK(ft)])
                else:
                    ADD("dve", lambda e, ft=ft: e.tensor_scalar(
                        out=xfm[:, ft, 0:T], in0=xfm[:, ft, 0:T], scalar1=vcol("ln_b", k * 8 + ft),
                        scalar2=None, op0=ALU.add),
                        reads=[XK(ft), ("vecs",)], writes=[XK(ft)])

            for ft in range(3):
                opA(ft)
            for ft in range(NFT):
                opB(ft)
                if ft + 3 < NFT:
                    opA(ft + 3)
                if ft >= 1:
                    opX(ft - 1)
                yield 1.7 * T / 512
            opX(NFT - 1)
            yield ("unlock", "ln")

        def stats(cfg, inv_n, eps_idx):
            T = cfg.T
            ADD("act", lambda e: e.activation(out=mean[:, 0:T], in_=ps[6][:, 0:T], func=AF.Identity, scale=inv_n),
                  reads=[("ps", 6)], writes=[("mean",)])
            ADD("dve", lambda e: e.tensor_tensor(out=sig[:, 1, 0:T], in0=mean[:, 0:T], in1=mean[:, 0:T], op=ALU.mult),
                  reads=[("mean",)], writes=[("sig", 1)])
            ADD("dve", lambda e: e.scalar_tensor_tensor(out=rstd[:, 0:T], in0=ps[7][:, 0:T], scalar=inv_n,
                                                          in1=sig[:, 1, 0:T], op0=ALU.mult, op1=ALU.subtract),
                  reads=[("ps", 7), ("sig", 1)], writes=[("rstd",)])
            ADD("act", lambda e: e.activation(out=rstd[:, 0:T], in_=rstd[:, 0:T], func=AF.Sqrt,
                                                bias=vcol("eps", eps_idx)),
                  reads=[("rstd",), ("vecs",)], writes=[("rstd",)])
            ADD("dve", lambda e: e.reciprocal(out=rstd[:, 0:T], in_=rstd[:, 0:T]),
                  reads=[("rstd",)], writes=[("rstd",)])

        def ffn(cfg, k):
            T = cfg.T
            f = 0 if k == 0 else 1
            xfm = xfm2[cfg.par]
            XK = lambda ft_: ("xfm", cfg.par, ft_)
            uin, ukn_ = (u, "u") if k == 0 else (u2, "u2")
            ukeys = [(ukn_, ft) for ft in range(NFT)]
            for j in range(NKF):
                c = j // 2
                slot = wslotF(cfg.tidx, k, c)
                wv = ringF[:, slot, :].rearrange("p (k m c) -> p k m c", k=8, m=4)
                bg = (j % 2) * 2
                bv = bg + 1
                for which, bank in ((0, bg), (1, bv)):
                    mt = (j % 2) * 2 + which

                    def _mm(e, wv=wv, mt=mt, bank=bank):
                        ins = None
                        for kt in range(NFT):
                            ins = e.matmul(out=ps[bank][:, 0:T], lhsT=wv[:, kt, mt, :], rhs=uin[:, kt, 0:T],
                                           start=(kt == 0), stop=(kt == NFT - 1))
                        return ins
                    ADD("pe", _mm, reads=ukeys + [("ringF", slot)], writes=[("ps", bank)])
                r = j % 2
                ADD("act", lambda e, r=r, bg=bg: e.activation(out=sg[:, r, 0:T], in_=ps[bg][:, 0:T], func=AF.Silu),
                      reads=[("ps", bg)], writes=[("sg", r)])
                ADD("dve", lambda e, r=r, bv=bv, j=j: e.tensor_tensor(out=h[:, j, 0:T], in0=sg[:, r, 0:T],
                                                                         in1=ps[bv][:, 0:T], op=ALU.mult),
                      reads=[("sg", r), ("ps", bv)], writes=[("h", j)])
                yield 3.9 * (0.5 + 0.5 * T / 512)
            if k == 2:
                yield ("mark", ("f2in", cfg.tidx))
            for mo in range(NFT):
                slot = wslotF(cfg.tidx, k, 11 + mo)
                wv = ringF[:, slot, 0:DFF].rearrange("p (k c) -> p k c", k=NKF)
                bank = 4 + (mo % 2)
                for half in range(2):
                    k0, k1 = (0, 11) if half == 0 else (11, 22)

                    def _mm(e, wv=wv, bank=bank, k0=k0, k1=k1):
                        ins = None
                        for kf in range(k0, k1):
                            ins = e.matmul(out=ps[bank][:, 0:T], lhsT=wv[:, kf, :], rhs=h[:, kf, 0:T],
                                           start=(kf == 0), stop=(kf == NKF - 1))
                        return ins
                    ADD("pe", _mm, reads=[("h", kf) for kf in range(k0, k1)] + [("ringF", slot)],
                          writes=[("ps", bank)])
                for (si, s, a, b) in segs(cfg):
                    ADD("dve", lambda e, mo=mo, bank=bank, s=s, a=a, b=b: e.scalar_tensor_tensor(
                        out=xfm[:, mo, a:b], in0=ps[bank][:, a:b], scalar=pc(P_gp[k][s], mo),
                        in1=xfm[:, mo, a:b], op0=ALU.mult, op1=ALU.add),
                        reads=[("ps", bank), XK(mo), ("par", k)], writes=[XK(mo)])
                yield 5.4 * (0.5 + 0.5 * T / 512)

        def seg_view(ap2d, nseg, w, lo, hi):
            return ap2d[:, 0:nseg * w].rearrange("p (s w) -> p s w", s=nseg)[:, :, lo:hi]

        def mixer(cfg):
            T, nseg, L = cfg.T, cfg.nseg, cfg.L
            xfm = xfm2[cfg.par]
            XK = lambda ft_: ("xfm", cfg.par, ft_)
            CWD = CH + L
            PWD = PH + L
            allcin = [("cin", i) for i in range(4)]
            allpin = [("pin", i) for i in range(4)]
            if cfg.kind == "S":
                for i in range(4):
                    ADD("sp", lambda e, i=i: e.dma_start(
                        out=cin[:, i, 0:2 * CWD].rearrange("p (s w) -> p s w", s=2)[:, :, 0:CH],
                        in_=cc_d[:, i * 2 * CH:(i + 1) * 2 * CH].rearrange("p (s t) -> p s t", s=2)),
                        writes=[("cin", i)], dma=("d_cc", i))
                    ADD("sp", lambda e, i=i: e.dma_start(
                        out=pin[:, i, 0:2 * PWD].rearrange("p (s w) -> p s w", s=2)[:, :, 0:PH],
                        in_=pc_d[:, i * 2 * PH:(i + 1) * 2 * PH].rearrange("p (s t) -> p s t", s=2)),
                        writes=[("pin", i)], dma=("d_pc", i))
                ADD("pool", lambda e: e.tensor_copy(out=cin[:, :, 2 * CWD:2 * CWD + CH], in_=cin[:, :, 512:512 + CH]),
                      reads=allcin, writes=allcin)
                ADD("pool", lambda e: e.tensor_copy(out=pin[:, :, 2 * PWD:2 * PWD + PH], in_=pin[:, :, 512:512 + PH]),
                      reads=allpin, writes=allpin)
            elif cfg.kind == "P0":
                ADD("pool", lambda e: e.memset(cin[:, :, 0:CH], 0.0), writes=allcin)
                ADD("pool", lambda e: e.memset(pin[:, :, 0:PH], 0.0), writes=allpin)
            else:
                ADD("pool", lambda e: e.tensor_copy(out=cin[:, :, 0:CH], in_=cin[:, :, 512:512 + CH]),
                      reads=allcin, writes=allcin)
                ADD("pool", lambda e: e.tensor_copy(out=pin[:, :, 0:PH], in_=pin[:, :, 512:512 + PH]),
                      reads=allpin, writes=allpin)
            ukeys = [("act", ft) for ft in range(NFT)]

            def inproj(local_chunk, mt, bank):
                slot = wslotM(cfg.tidx, CH_WMI + local_chunk)
                wv = ringM[:, slot, :].rearrange("p (k m c) -> p k m c", k=8, m=4)

                def _mm(e, wv=wv, mt=mt, bank=bank):
                    ins = None
                    for kt in range(NFT):
                        ins = e.matmul(out=ps[bank][:, 0:T], lhsT=wv[:, kt, mt, :], rhs=act[:, kt, 0:T],
                                       start=(kt == 0), stop=(kt == NFT - 1))
                    return ins
                ADD("pe", _mm, reads=ukeys + [("ringM", slot)], writes=[("ps", bank)])

            for i in range(4):
                c = i // 2
                bb = (i % 2) * 2
                ba = bb + 1
                inproj(c, (i % 2) * 2, bb)
                inproj(c, (i % 2) * 2 + 1, ba)
                r = 0
                mb = _off["mixb"] + c * 4 + (i % 2) * 2
                hbc = P_hb + c * 4 + (i % 2) * 2
                ADD("act", lambda e, r=r, bb=bb, hbc=hbc: e.activation(
                    out=sig[:, r, 0:T], in_=ps[bb][:, 0:T], func=AF.Tanh, scale=0.5, bias=par[:, hbc:hbc + 1]),
                    reads=[("ps", bb), ("par", "hb")], writes=[("sig", r)])
                ADD("dve", lambda e, r=r: e.tensor_scalar(
                    out=sig[:, r, 0:T], in0=sig[:, r, 0:T], scalar1=1.0, scalar2=0.5, op0=ALU.add, op1=ALU.mult),
                    reads=[("sig", r)], writes=[("sig", r)])
                ADD("dve", lambda e, i=i, r=r, ba=ba, mb=mb: e.scalar_tensor_tensor(
                    out=seg_view(cin[:, i, :], nseg, CWD, CH, CWD),
                    in0=ps[ba][:, 0:T].rearrange("p (s l) -> p s l", s=nseg),
                    scalar=vecs[:, mb + 1:mb + 2],
                    in1=sig[:, r, 0:T].rearrange("p (s l) -> p s l", s=nseg),
                    op0=ALU.add, op1=ALU.mult),
                    reads=[("ps", ba), ("sig", r), ("vecs",)], writes=[("cin", i)])
                yield 4.0 * (0.5 + 0.5 * T / 512)
            for i in range(4):
                bank = i % 4
                inproj(2, i, bank)
                mb = _off["mixb"] + 8 + i
                ADD("act", lambda e, i=i, bank=bank, mb=mb: e.activation(
                    out=seg_view(pin[:, i, :], nseg, PWD, PH, PWD),
                    in_=ps[bank][:, 0:T].rearrange("p (s l) -> p s l", s=nseg),
                    func=AF.Identity, bias=vecs[:, mb:mb + 1]),
                    reads=[("ps", bank), ("vecs",)], writes=[("pin", i)])
                yield 2.0 * (0.5 + 0.5 * T / 512)
            if cfg.kind == "P0":
                ADD("dve", lambda e: e.tensor_scalar(out=cin[:, :, CH:CH + 64], in0=cin[:, :, CH:CH + 64],
                                                       scalar1=vcol("mask", 0), scalar2=None, op0=ALU.mult),
                      reads=allcin + [("vecs",)], writes=allcin)
                ADD("dve", lambda e: e.tensor_scalar(out=pin[:, :, PH:PH + 64], in0=pin[:, :, PH:PH + 64],
                                                       scalar1=vcol("mask", 0), scalar2=None, op0=ALU.mult),
                      reads=allpin + [("vecs",)], writes=allpin)
            if cfg.kind == "S":
                emit_state(cfg, 0, scs_d[0], sps_d[0])
                emit_state(cfg, 1, scs_d[1], sps_d[1])
                emit_state(cfg, 2, scp_d, spp_d)
            W = PWD
            for g in range(4):
                pv = lambda lo, hi, g=g: seg_view(pin[:, g, :], nseg, W, lo, hi)
                av_ = lambda lo, hi: seg_view(sA[:], nseg, W, lo, hi)
                bv_ = lambda lo, hi: seg_view(sB[:], nseg, W, lo, hi)
                ADD("pool", lambda e, pv=pv, av_=av_: e.tensor_tensor(
                    out=av_(1, W), in0=pv(1, W), in1=pv(0, W - 1), op=ALU.add),
                    reads=[("pin", g)], writes=[("sA",)])
                last = av_
                lastkey = ("sA",)
                if g >= 1:
                    ADD("pool", lambda e, av_=av_, bv_=bv_: e.tensor_tensor(
                        out=bv_(3, W), in0=av_(3, W), in1=av_(1, W - 2), op=ALU.add),
                        reads=[("sA",)], writes=[("sB",)])
                    last, lastkey = bv_, ("sB",)
                if g >= 2:
                    ADD("pool", lambda e, av_=av_, bv_=bv_: e.tensor_tensor(
                        out=av_(7, W), in0=bv_(7, W), in1=bv_(3, W - 4), op=ALU.add),
                        reads=[("sB",)], writes=[("sA",)])
                    last, lastkey = av_, ("sA",)
                if g >= 3:
                    ADD("pool", lambda e, av_=av_, bv_=bv_: e.tensor_tensor(
                        out=bv_(15, W), in0=av_(15, W), in1=av_(7, W - 8), op=ALU.add),
                        reads=[("sA",)], writes=[("sB",)])
                    last, lastkey = bv_, ("sB",)
                wnd = float(2 ** (g + 1))
                ADD("dve", lambda e, g=g, last=last, pv=pv, wnd=wnd: e.scalar_tensor_tensor(
                    out=pooled[:, g, 0:T].rearrange("p (s l) -> p s l", s=nseg),
                    in0=last(PH, W), scalar=1.0 / wnd, in1=pv(PH, W), op0=ALU.mult, op1=ALU.subtract),
                    reads=[lastkey, ("pin", g)], writes=[("pooled", g)])
                if cfg.kind == "P0":
                    ic = _off["invcnt"] + g * 16
                    lastflat = sA if lastkey == ("sA",) else sB
                    ADD("pool", lambda e, lastflat=lastflat, ic=ic: e.tensor_tensor(
                        out=t16[:], in0=lastflat[:, PH + 64:PH + 80], in1=vecs[:, ic:ic + 16], op=ALU.mult),
                        reads=[lastkey, ("vecs",)], writes=[("t16",)])
                    ADD("pool", lambda e, g=g: e.tensor_tensor(
                        out=pooled[:, g, 64:80], in0=t16[:], in1=pin[:, g, PH + 64:PH + 80], op=ALU.subtract),
                        reads=[("t16",), ("pin", g), ("pooled", g)], writes=[("pooled", g)])
                slot = wslotM(cfg.tidx, CH_WPO)
                wv = ringM[:, slot, 0:512].rearrange("p (g d) -> p g d", g=4)
                bank = 4 + (g % 2)
                ADD("pe", lambda e, g=g, wv=wv, bank=bank: e.matmul(
                    out=ps[bank][:, 0:T], lhsT=wv[:, g, :], rhs=pooled[:, g, 0:T], start=True, stop=True),
                    reads=[("pooled", g), ("ringM", slot)], writes=[("ps", bank)])
                ADD("dve", lambda e, g=g, bank=bank: e.tensor_scalar(
                    out=act[:, 4 + g, 0:T], in0=ps[bank][:, 0:T], scalar1=vcol("pool_b", g),
                    scalar2=vcol("pool_s", g), op0=ALU.add, op1=ALU.mult),
                    reads=[("ps", bank), ("vecs",)], writes=[("act", 4 + g)])
                yield 1.5 * T / 512
            cvs = [(lambda k, i=i: seg_view(cin[:, i, :], nseg, CWD, k, k + L)) for i in range(4)]
            avs = [acc[:, i, 0:T].rearrange("p (s l) -> p s l", s=nseg) for i in range(4)]
            for i in range(4):
                ADD("dve", lambda e, i=i: e.tensor_scalar(
                    out=avs[i], in0=cvs[i](0), scalar1=vcol("conv_w", 0 * 4 + i), scalar2=vcol("conv_b", i),
                    op0=ALU.mult, op1=ALU.add),
                    reads=[("cin", i), ("vecs",)], writes=[("acc", i)])
                yield 0.4 * T / 512
            for k in range(1, CW):
                for i in range(4):
                    ADD("dve", lambda e, i=i, k=k: e.scalar_tensor_tensor(
                        out=avs[i], in0=cvs[i](k), scalar=vcol("conv_w", k * 4 + i), in1=avs[i],
                        op0=ALU.mult, op1=ALU.add),
                        reads=[("cin", i), ("acc", i), ("vecs",)], writes=[("acc", i)])
                    yield 0.62 * T / 512
            yield ("lock", "ln")
            for i in range(4):
                ADD("dve", lambda e, i=i: e.tensor_copy(out=rr[:, i % 2, 0:T], in_=acc[:, i, 0:T]),
                      reads=[("acc", i)], writes=[("rr", i % 2)])
                ADD("act", lambda e, i=i: e.activation(out=sq[:, i % 2, 0:T], in_=acc[:, i, 0:T], func=AF.Square),
                      reads=[("acc", i)], writes=[("sq", i % 2)])

                def _st(e, i=i):
                    e.matmul(out=ps[6][:, 0:T], lhsT=ones[:], rhs=rr[:, i % 2, 0:T],
                             start=(i == 0), stop=(i == 3))
                    return e.matmul(out=ps[7][:, 0:T], lhsT=ones[:], rhs=sq[:, i % 2, 0:T],
                                    start=(i == 0), stop=(i == 3))
                ADD("pe", _st, reads=[("rr", i % 2), ("sq", i % 2), ("ones",)], writes=[("ps", 6), ("ps", 7)])
                yield 0.6 * T / 512
            stats(cfg, 1.0 / 512, 1)
            yield 6.0
            for i in range(4):
                ADD("dve", lambda e, i=i: e.tensor_tensor(out=acc[:, i, 0:T], in0=acc[:, i, 0:T],
                                                             in1=mean[:, 0:T], op=ALU.subtract),
                      reads=[("acc", i), ("mean",)], writes=[("acc", i)])
                ADD("dve", lambda e, i=i: e.scalar_tensor_tensor(
                    out=acc[:, i, 0:T], in0=acc[:, i, 0:T], scalar=vcol("cln_g", i), in1=rstd[:, 0:T],
                    op0=ALU.mult, op1=ALU.mult),
                    reads=[("acc", i), ("rstd",), ("vecs",)], writes=[("acc", i)])
                ADD("act", lambda e, i=i: e.activation(out=act[:, i, 0:T], in_=acc[:, i, 0:T], func=AF.Silu,
                                                         bias=vcol("cln_b", i)),
                      reads=[("acc", i), ("vecs",)], writes=[("act", i)])
                yield 1.9 * T / 512
            yield ("unlock", "ln")
            akeys = [("act", kk) for kk in range(8)]
            for mo in range(NFT):
                slot = wslotM(cfg.tidx, CH_WO + mo // 4)
                wv = ringM[:, slot, :].rearrange("p (m k c) -> p m k c", m=4, k=8)
                bank = 4 + (mo % 2)

                def _mm(e, wv=wv, mo=mo, bank=bank):
                    ins = None
                    for kk in range(8):
                        ins = e.matmul(out=ps[bank][:, 0:T], lhsT=wv[:, mo % 4, kk, :], rhs=act[:, kk, 0:T],
                                       start=(kk == 0), stop=(kk == 7))
                    return ins
                ADD("pe", _mm, reads=akeys + [("ringM", slot)], writes=[("ps", bank)])
                for (si, s, a, b) in segs(cfg):
                    ADD("dve", lambda e, mo=mo, bank=bank, s=s, a=a, b=b: e.scalar_tensor_tensor(
                        out=xfm[:, mo, a:b], in0=ps[bank][:, a:b], scalar=pc(P_gp[1][s], mo),
                        in1=xfm[:, mo, a:b], op0=ALU.mult, op1=ALU.add),
                        reads=[("ps", bank), XK(mo), ("par", 1)], writes=[XK(mo)])
                yield 2.0 * (0.5 + 0.5 * T / 512)

        st_count = [0]

        def emit_state(cfg, seg, conv_dst, pool_dst):
            L = cfg.L
            CWD = CH + L
            PWD = PH + L
            for (buf, key, wd, hist, dst, dkey) in ((cin, "cin", CWD, CH, conv_dst, "stc"),
                                                     (pin, "pin", PWD, PH, pool_dst, "stp")):
                r = st_count[0] % 2
                st_count[0] += 1
                bank = 4 + r
                c0 = seg * wd + L

                def _tr(e, buf=buf, bank=bank, c0=c0, hist=hist):
                    ins = None
                    for i in range(4):
                        ins = e.transpose(out=ps[bank][:hist, i * 128:(i + 1) * 128],
                                          in_=buf[:, i, c0:c0 + hist], identity=ident)
                    return ins
                ADD("pe", _tr, reads=[(key, i) for i in range(4)] + [("vecs",)], writes=[("ps", bank)])
                ADD("act", lambda e, bank=bank, r=r, hist=hist: e.activation(
                    out=yt[:hist, r, 0:512], in_=ps[bank][:hist, :], func=AF.Copy),
                    reads=[("ps", bank)], writes=[("yt", r)])
                ADD("sp", lambda e, r=r, hist=hist, dst=dst: e.dma_start(out=dst, in_=yt[:hist, r, 0:512]),
                      reads=[("yt", r)], dma=("st", r))

        def store_out(cfg):
            xfm = xfm2[cfg.par]
            XK = lambda ft_: ("xfm", cfg.par, ft_)
            for (bi, dst) in cfg.outblocks:
                c0, rows = cfg.blocks[bi]
                r = bi % 2
                for half in range(2):
                    bank = 4 + half

                    def _tr(e, half=half, bank=bank, c0=c0, rows=rows):
                        ins = None
                        for q in range(4):
                            ft = half * 4 + q
                            ins = e.transpose(out=ps[bank][:rows, q * 128:(q + 1) * 128],
                                              in_=xfm[:, ft, c0:c0 + rows], identity=ident)
                        return ins
                    ADD("pe", _tr, reads=[XK(half * 4 + q) for q in range(4)] + [("vecs",)],
                          writes=[("ps", bank)])
                    ADD("act", lambda e, half=half, bank=bank, rows=rows, r=r: e.activation(
                        out=yt[:rows, r, half * 512:(half + 1) * 512], in_=ps[bank][:rows, :], func=AF.Copy),
                        reads=[("ps", bank)], writes=[("yt", r)])
                ADD("sp", lambda e, rows=rows, r=r, dst=dst: e.dma_start(out=dst, in_=yt[:rows, r, :]),
                      reads=[("yt", r)], dma=("yout", r))
                yield 2.0

        tiles = []
        tidx = 0
        for n in range(n_ptiles):
            kind = "P0" if n == 0 else "P"
            xsrc = xp_d[n * 512:(n + 1) * 512, :]
            outb = [(bi, yp_d[n * 512 + bi * 128:n * 512 + (bi + 1) * 128, :]) for bi in range(4)]
            tiles.append(TileCfg(f"P{n}", 512, 1, 512, [0], [(bi * 128, 128) for bi in range(4)], xsrc,
                                 outb, kind, tidx))
            tidx += 1
        if do_s:
            tiles.append(TileCfg("S", 192, 3, 64, [1, 2, 0], [(0, 128), (128, 64)], xs_d,
                                 [(0, ys_d[0:128, :]), (1, ys_d[128:192, :])], "S", tidx))
            tidx += 1
        N = len(tiles)
        for t_ in tiles:
            t_.par = t_.tidx % 2
        f_order.append((0, 0))
        for n in range(N):
            if n >= 1:
                f_order.append((n - 1, 2))
            if n + 1 < N:
                f_order.append((n + 1, 0))
        f_order.append((N - 1, 2))
        for i_, tk in enumerate(f_order):
            f_index[tk] = i_
        wst["ntiles"] = N

        def chain(*gs):
            for g in gs:
                yield from g

        marks = set([("f2in", -1)])

        def run(g):
            for c in g:
                if isinstance(c, tuple) and c[0] == "mark" and not dry[0]:
                    marks.add(c[1])

        def total(mk):
            dry[0] = True
            t = sum(c for c in mk() if not isinstance(c, tuple))
            dry[0] = False
            return max(t, 1e-6)

        def merge(mka, mkb, bbias=1.0):
            Ta, Tb = total(mka), total(mkb)
            gens = {"a": mka(), "b": mkb()}
            t = {"a": 0.0, "b": 0.0}
            T_ = {"a": Ta, "b": Tb * bbias}
            done = {"a": False, "b": False}
            need = {"a": None, "b": None}
            wantlock = {"a": False, "b": False}
            holder = [None]

            def blocked(sd):
                if need[sd] is not None and need[sd] not in marks:
                    return True
                if wantlock[sd]:
                    if holder[0] is None:
                        holder[0] = sd
                        wantlock[sd] = False
                        return False
                    return holder[0] != sd
                return False

            while not (done["a"] and done["b"]):
                cands = [sd for sd in ("a", "b") if not done[sd] and not blocked(sd)]
                if not cands:
                    raise RuntimeError("merge deadlock")
                if len(cands) == 2 and holder[0] == "a":
                    sd = "a"
                elif len(cands) == 2:
                    sd = "a" if t["a"] / T_["a"] <= t["b"] / T_["b"] else "b"
                else:
                    sd = cands[0]
                need[sd] = None
                try:
                    c = next(gens[sd])
                except StopIteration:
                    done[sd] = True
                    if holder[0] == sd:
                        holder[0] = None
                    continue
                if isinstance(c, tuple):
                    if c[0] == "mark":
                        marks.add(c[1])
                    elif c[0] == "need":
                        need[sd] = c[1]
                    elif c[0] == "lock":
                        if holder[0] is None or holder[0] == sd:
                            holder[0] = sd
                        else:
                            wantlock[sd] = True
                    elif c[0] == "unlock":
                        if holder[0] == sd:
                            holder[0] = None
                else:
                    t[sd] += c

        run(g_ada(0, 12))
        t0 = tiles[0]
        merge(lambda: chain(load_in(t0), ffn(t0, 0)), lambda: g_ada(12, 36))
        def tail(t_):
            return chain(ffn(t_, 2), layer_norm(t_, 2), store_out(t_))

        def head(t_):
            return chain(load_in(t_), ffn(t_, 0))

        for n in range(N):
            tn = tiles[n]
            mkB = lambda tn=tn: chain(layer_norm(tn, 0), mixer(tn), layer_norm(tn, 1))
            parts = []
            if n >= 1:
                parts.append(lambda n=n: tail(tiles[n - 1]))
            if n + 1 < N:
                parts.append(lambda n=n: head(tiles[n + 1]))
            mkA = lambda parts=parts: chain(*[p() for p in parts])
            if parts and interleave:
                merge(mkA, mkB, bbias=1.04)
            else:
                run(mkA())
                run(mkB())
        run(tail(tiles[N - 1]))

        S.finalize()
        out_keys = [k for k in S.dma_counts if isinstance(k, tuple) and k[0] in ("yout", "st")]
        sem_es = contextlib.ExitStack()
        with sem_es:
            eng_sem = {e: sem_es.enter_context(nc.semaphore(f"sem_{e}")) for e in Sched.ENG}
            dma_sem = {}
            for i, key in enumerate(S.dma_counts):
                dma_sem[key] = sem_es.enter_context(nc.semaphore(f"dsem_{i}"))
            with nc.Block() as block:
                @block.sync
                def _(e):
                    S.emit_engine("sp", e, eng_sem, dma_sem, final_waits=out_keys)

                @block.tensor
                def _(e):
                    S.emit_engine("pe", e, eng_sem, dma_sem)

                @block.scalar
                def _(e):
                    S.emit_engine("act", e, eng_sem, dma_sem)

                @block.vector
                def _(e):
                    S.emit_engine("dve", e, eng_sem, dma_sem)

                @block.gpsimd
                def _(e):
                    S.emit_engine("pool", e, eng_sem, dma_sem)
    return nc, S


def _vec_tiles(v, nt):
    return np.ascontiguousarray(np.asarray(v, np.float32).reshape(nt, 128).T)


def prep_weights(inp):
    f32 = np.float32
    w_in = np.asarray(inp["ffn_w_in"], f32)[0]
    w1 = np.empty((2, 11, 128, 4096), f32)
    ct_order = []
    for c in range(11):
        for mt in range(4):
            j = 2 * c + mt // 2
            ct_order.append(j if mt % 2 == 0 else 22 + j)
    for f in range(2):
        W = w_in[f].reshape(8, 128, 44, 128)[:, :, ct_order, :]
        W = W.reshape(8, 128, 11, 4, 128).transpose(2, 1, 0, 3, 4)
        w1[f] = W.reshape(11, 128, 4096)
    w_dn = np.asarray(inp["ffn_w_down"], f32)[0]
    wd = np.empty((2, 8, 128, DFF), f32)
    for f in range(2):
        W = w_dn[f].reshape(22, 128, 8, 128).transpose(2, 1, 0, 3)
        wd[f] = W.reshape(8, 128, DFF)
    wmi_src = np.asarray(inp["mix_w_in"], f32)[0]
    W = wmi_src.reshape(8, 128, 12, 128)[:, :, MIX_CT, :]
    W = W.reshape(8, 128, 3, 4, 128).transpose(2, 1, 0, 3, 4)
    wmi = np.ascontiguousarray(W.reshape(3, 128, 4096))
    wpo = np.ascontiguousarray(np.asarray(inp["pool_w"], f32)[0].transpose(1, 0, 2).reshape(128, 512))
    wo_src = np.asarray(inp["mix_w_out"], f32)[0]
    W = wo_src.reshape(8, 128, 8, 128).transpose(2, 1, 0, 3)
    W = W.reshape(2, 4, 128, 8, 128).transpose(0, 2, 1, 3, 4)
    wo = np.ascontiguousarray(W.reshape(2, 128, 4096))
    ada_src = np.asarray(inp["ada_w"], f32)[0]
    ada = np.ascontiguousarray(ada_src.reshape(8, 128, 36, 256).transpose(2, 1, 0, 3).reshape(36, 128, 2048))
    return dict(w1=w1, wd=wd, wmi=wmi, wpo=wpo, wo=wo, ada_w=ada)


def prep_core(inp, i, shared_cols, npt=8):
    f32 = np.float32
    b, q = i // 4, i % 4
    xpr = np.asarray(inp["x_prompt"], f32)
    xsm = np.asarray(inp["x_sample"], f32)
    nmain = 512 * npt - 64
    if q > 0:
        halo = xpr[b, q * PCH - 64:q * PCH]
    else:
        halo = np.zeros((64, D), f32)
    xp = np.zeros((PCH, D), f32)
    xp[0:64] = halo
    xp[64:64 + nmain] = xpr[b, q * PCH:q * PCH + nmain]
    xs = np.ascontiguousarray(np.concatenate(
        [xsm[2 * i], xsm[2 * i + 1], xpr[b, q * PCH + nmain:q * PCH + nmain + 64]], axis=0))
    c3 = np.stack([np.asarray(inp["c_prompt"], f32)[b], np.asarray(inp["c_sample"], f32)[2 * i],
                   np.asarray(inp["c_sample"], f32)[2 * i + 1], np.asarray(inp["c_prompt"], f32)[b]], axis=0)
    cT = np.ascontiguousarray(c3.T.reshape(8, 128, NSEQ).transpose(1, 0, 2).reshape(128, 8 * NSEQ))
    cc = np.asarray(inp["cache_conv"], f32)[0, 2 * i:2 * i + 2]
    ccache = np.ascontiguousarray(cc.reshape(2, CH, 4, 128).transpose(3, 2, 0, 1).reshape(128, 4 * 2 * CH))
    pcs = np.asarray(inp["cache_pool"], f32)[0, 2 * i:2 * i + 2]
    pcache = np.ascontiguousarray(pcs.reshape(2, PH, 4, 128).transpose(3, 2, 0, 1).reshape(128, 4 * 2 * PH))
    vecs = shared_cols.copy()
    vecs[:, _off["mask"]] = 0.0 if q == 0 else 1.0
    inv = np.empty((4, 16), f32)
    for g in range(4):
        w = 2 ** (g + 1)
        for t in range(16):
            inv[g, t] = 1.0 / (min(t + 1, w) if q == 0 else w)
    vecs[:, _off["invcnt"]:_off["invcnt"] + 64] = inv.reshape(1, 64)
    return dict(xs=xs, xp=xp, cT=cT, ccache=ccache, pcache=pcache, vecs=vecs)


def shared_vecs(inp):
    f32 = np.float32
    v = np.zeros((128, NV), f32)
    v[:, _off["ident"]:_off["ident"] + 128] = np.eye(128, dtype=f32)
    v[:, _off["ada_b"]:_off["ada_b"] + 72] = _vec_tiles(np.asarray(inp["ada_b"])[0], 72)
    v[:, _off["ln_g"]:_off["ln_g"] + 24] = _vec_tiles(np.asarray(inp["ln_g"])[0].reshape(-1), 24)
    v[:, _off["ln_b"]:_off["ln_b"] + 24] = _vec_tiles(np.asarray(inp["ln_b"])[0].reshape(-1), 24)
    mb = _vec_tiles(np.asarray(inp["mix_b_in"])[0], 12)[:, MIX_CT]
    v[:, _off["mixb"]:_off["mixb"] + 12] = mb
    cw = np.asarray(inp["conv_w"], f32)[0]
    v[:, _off["conv_w"]:_off["conv_w"] + 124] = cw.reshape(31, 4, 128).transpose(2, 0, 1).reshape(128, 124)
    v[:, _off["conv_b"]:_off["conv_b"] + 4] = _vec_tiles(np.asarray(inp["conv_b"])[0], 4)
    v[:, _off["cln_g"]:_off["cln_g"] + 4] = _vec_tiles(np.asarray(inp["conv_ln_g"])[0], 4)
    v[:, _off["cln_b"]:_off["cln_b"] + 4] = _vec_tiles(np.asarray(inp["conv_ln_b"])[0], 4)
    v[:, _off["pool_b"]:_off["pool_b"] + 4] = _vec_tiles(np.asarray(inp["pool_b"])[0].reshape(-1), 4)
    v[:, _off["pool_s"]:_off["pool_s"] + 4] = _vec_tiles(np.asarray(inp["pool_scale"])[0], 4)
    v[:, _off["b_out"]:_off["b_out"] + 8] = _vec_tiles(np.asarray(inp["mix_b_out"])[0], 8)
    v[:, _off["eps"]] = EPS / (ALPHA * ALPHA)
    v[:, _off["eps"] + 1] = EPS
    return v


_NC_CACHE = {}


def kernel(**inputs):
    if "nc" not in _NC_CACHE:
        _NC_CACHE["nc"] = build_program()[0]
    nc = _NC_CACHE["nc"]
    wts = prep_weights(inputs)
    sv = shared_vecs(inputs)
    in_maps = []
    for i in range(NCORES):
        m = prep_core(inputs, i, sv)
        m.update(wts)
        in_maps.append(m)
    res = run_bass_kernel_spmd(nc, in_maps, core_ids=list(range(NCORES)))
    r = res.results
    f32 = np.float32
    y_prompt = np.empty((2, 16384, D), f32)
    y_sample = np.empty((16, 64, D), f32)
    scp = np.empty((1, 2, CH, 512), f32)
    spp = np.empty((1, 2, PH, 512), f32)
    scs = np.empty((1, 16, CH, 512), f32)
    sps = np.empty((1, 16, PH, 512), f32)
    for i in range(NCORES):
        b, q = i // 4, i % 4
        y_prompt[b, q * PCH:(q + 1) * PCH - 64] = r[i]["yp"][64:PCH]
        y_prompt[b, (q + 1) * PCH - 64:(q + 1) * PCH] = r[i]["ys"][128:192]
        y_sample[2 * i] = r[i]["ys"][0:64]
        y_sample[2 * i + 1] = r[i]["ys"][64:128]
        scs[0, 2 * i:2 * i + 2] = r[i]["scs"]
        sps[0, 2 * i:2 * i + 2] = r[i]["sps"]
        if q == 3:
            scp[0, b] = r[i]["scp"]
            spp[0, b] = r[i]["spp"]
    return (y_prompt, y_sample, scp, spp, scs, sps)
```

```python
import contextlib
import numpy as np
import concourse.bass as bass
import concourse.mybir as mybir
from concourse.bass_utils import run_bass_kernel_spmd

F32 = mybir.dt.float32
BF16 = mybir.dt.bfloat16
F32R = mybir.dt.float32r
AF = mybir.ActivationFunctionType
ALU = mybir.AluOpType

D = 1024
DFF = 2816
NFT = 8
NKF = 22
CW = 31
CH = 30
PH = 15
NCORES = 8
PCH = 4096
ALPHA = 2.0 ** 0.25
EPS = 1e-5
NSF = 3
NSM = 2
SLOT_ELEMS = 4096
NSEQ = 4

_off = {}
_o = 0
for _n, _w in [("ident", 128), ("ada_b", 72), ("ln_g", 24), ("ln_b", 24), ("mixb", 12),
               ("conv_w", 124), ("conv_b", 4), ("cln_g", 4), ("cln_b", 4), ("pool_b", 4),
               ("pool_s", 4), ("b_out", 8), ("mask", 1), ("invcnt", 64), ("eps", 2)]:
    _off[_n] = _o
    _o += _w
NV = _o

MIX_CT = [4, 0, 5, 1, 6, 2, 7, 3, 8, 9, 10, 11]


class _Op:
    __slots__ = ("eng", "fn", "deps", "pos", "signal", "count", "dma_key", "dma_val", "is_dma")


class Sched:
    ENG = ("pe", "act", "dve", "pool", "sp")

    def __init__(self):
        self.ops = []
        self.lastw = {}
        self.readers = {}
        self.eng_ops = {e: [] for e in self.ENG}
        self.dma_counts = {}

    def add(self, eng, fn, reads=(), writes=(), dma=None):
        op = _Op()
        op.eng = eng
        op.fn = fn
        op.is_dma = dma is not None
        op.signal = False
        op.count = 0
        deps = set()
        writes = list(writes) + [k for k in reads if k[0] == "ps"]
        reads = [k for k in reads if k[0] != "ps"]
        for k in reads:
            w = self.lastw.get(k)
            if w is not None:
                deps.add(w)
        for k in writes:
            w = self.lastw.get(k)
            if w is not None:
                deps.add(w)
            for r in self.readers.get(k, ()):
                deps.add(r)
        for k in reads:
            self.readers.setdefault(k, []).append(op)
        for k in writes:
            self.lastw[k] = op
            self.readers[k] = []
        op.deps = deps
        op.pos = len(self.eng_ops[eng])
        self.eng_ops[eng].append(op)
        if dma is not None:
            n = self.dma_counts.get(dma, 0) + 1
            self.dma_counts[dma] = n
            op.dma_key = dma
            op.dma_val = 16 * n
        else:
            op.dma_key = None
            op.dma_val = 0
        self.ops.append(op)
        return op

    @staticmethod
    def _needs(op, d):
        if d.is_dma:
            return True
        if d.eng != op.eng:
            return True
        if op.is_dma:
            return True
        if op.eng == "pe":
            return False
        return (op.pos - d.pos) <= 2

    def finalize(self):
        for op in self.ops:
            for d in op.deps:
                if not d.is_dma and self._needs(op, d):
                    d.signal = True
        for e in self.ENG:
            c = 0
            for op in self.eng_ops[e]:
                if op.signal:
                    c += 1
                op.count = c

    def emit_engine(self, e, eng, eng_sem, dma_sem, final_waits=()):
        waited = {}
        for op in self.eng_ops[e]:
            waits = {}
            for d in op.deps:
                if not self._needs(op, d):
                    continue
                if d.is_dma:
                    key = ("dma", d.dma_key)
                    val = d.dma_val
                else:
                    key = ("eng", d.eng)
                    val = d.count
                if val > waits.get(key, 0):
                    waits[key] = val
            for key, val in waits.items():
                if waited.get(key, 0) >= val:
                    continue
                waited[key] = val
                sem = dma_sem[key[1]] if key[0] == "dma" else eng_sem[key[1]]
                eng.wait_ge(sem, val)
            ins = op.fn(eng)
            if op.is_dma:
                ins.then_inc(dma_sem[op.dma_key], 16)
            elif op.signal:
                ins.then_inc(eng_sem[e], 1)
        for key in final_waits:
            eng.wait_ge(dma_sem[key], 16 * self.dma_counts[key])


class TileCfg:
    def __init__(self, name, T, nseg, L, segseq, blocks, xsrc, outblocks, kind, tidx):
        self.name = name
        self.T = T
        self.nseg = nseg
        self.L = L
        self.segseq = segseq
        self.blocks = blocks
        self.xsrc = xsrc
        self.outblocks = outblocks
        self.kind = kind
        self.tidx = tidx
        self.par = 0


def build_program(n_ptiles=8, do_s=True, stop_after=None, interleave=True):
    nc = bass.Bass("TRN2", target_bir_lowering=False)

    def din(name, shape):
        return nc.dram_tensor(name, shape, F32, kind="ExternalInput").ap()

    def dout(name, shape):
        return nc.dram_tensor(name, shape, F32, kind="ExternalOutput").ap()

    xs_d = din("xs", [192, D])
    xp_d = din("xp", [PCH, D])
    cT_d = din("cT", [128, NFT * NSEQ])
    cc_d = din("ccache", [128, 4 * 2 * CH])
    pc_d = din("pcache", [128, 4 * 2 * PH])
    vecs_d = din("vecs", [128, NV])
    ada_d = din("ada_w", [36, 128, 2048])
    w1_d = din("w1", [2, 11, 128, 4096])
    wd_d = din("wd", [2, 8, 128, DFF])
    wmi_d = din("wmi", [3, 128, 4096])
    wpo_d = din("wpo", [128, 512])
    wo_d = din("wo", [2, 128, 4096])

    yp_d = dout("yp", [PCH, D])
    ys_d = dout("ys", [192, D])
    scp_d = dout("scp", [CH, 512])
    spp_d = dout("spp", [PH, 512])
    scs_d = dout("scs", [2, CH, 512])
    sps_d = dout("sps", [2, PH, 512])

    S = Sched()
    dry = [False]

    class _Dummy:
        def add(self, *a, **k):
            return None
    _dummy = _Dummy()

    def ADD(*a, **k):
        return (_dummy if dry[0] else S).add(*a, **k)
    es = contextlib.ExitStack()

    def sb(name, shape, dt=F32):
        return es.enter_context(nc.sbuf_tensor("s_" + name, shape, dt))

    with es:
        vecs = sb("vecs", [128, NV])
        ones = sb("ones", [128, 128], BF16)
        cT = sb("cT", [128, NFT, NSEQ])
        sc = sb("sc", [128, NFT, NSEQ], BF16)
        modsb = sb("modsb", [128, 72, NSEQ])
        par = sb("par", [128, 264])
        xt = sb("xt", [128, 4, D])
        yt = sb("yt", [128, 2, D])
        xfm2 = [sb("xfm0", [128, NFT, 512]), sb("xfm1", [128, NFT, 512])]
        u = sb("u", [128, NFT, 512], BF16)
        h = sb("h", [128, NKF, 512], BF16)
        sg = sb("sg", [128, 2, 512])
        sq = sb("sq", [128, 2, 512], BF16)
        rr = sb("rr", [128, 2, 512], BF16)
        mean = sb("mean", [128, 512])
        rstd = sb("rstd", [128, 512])
        cin = sb("cin", [128, 4, 544])
        pin = sb("pin", [128, 4, 528])
        acc = sb("acc", [128, 4, 512])
        sig = sb("sig", [128, 2, 512])
        act = sb("act", [128, 8, 512], BF16)
        pooled = sb("pooled", [128, 4, 512], BF16)
        sA = sb("sA", [128, 528])
        sB = sb("sB", [128, 528])
        t16 = sb("t16", [128, 16])
        u2 = sb("u2", [128, NFT, 512], BF16)
        ringF = sb("ringF", [128, NSF, SLOT_ELEMS], BF16)
        ringM = sb("ringM", [128, NSM, SLOT_ELEMS], BF16)
        ps = [es.enter_context(nc.psum_tensor(f"ps{i}", [128, 512], F32)) for i in range(8)]

        ident = vecs[:, _off["ident"]:_off["ident"] + 128]

        def vcol(name, idx):
            o = _off[name] + idx
            return vecs[:, o:o + 1]

        pcol = [0]

        def palloc(n):
            o = pcol[0]
            pcol[0] += n
            return o

        P_s1 = [[palloc(8) for s in range(3)] for k in range(3)]
        P_gp = [[palloc(8) for s in range(3)] for k in range(3)]
        P_tB = [[palloc(8) for s in range(3)] for k in range(3)]
        P_xb0 = [palloc(8) for s in range(3)]
        P_tmp = palloc(8)
        P_hb = palloc(12)

        def pc(o, ft):
            return par[:, o + ft:o + ft + 1]

        def modv(j, s):
            return modsb[:, j * 8:(j + 1) * 8, s]

        def modc(j, ft, s):
            return modsb[:, j * 8 + ft, s:s + 1]

        ADD("sp", lambda e: e.dma_start(out=vecs[:], in_=vecs_d), writes=[("vecs",)], dma="d_vecs")
        ADD("sp", lambda e: e.dma_start(out=cT[:].rearrange("p k s -> p (k s)"), in_=cT_d),
              writes=[("cT",)], dma="d_cT")
        ADD("act", lambda e: e.activation(out=ones[:], in_=ident, func=AF.Identity, scale=0.0, bias=1.0),
              reads=[("vecs",)], writes=[("ones",)])
        ADD("dve", lambda e: e.tensor_scalar(out=par[:, P_hb:P_hb + 12],
                                             in0=vecs[:, _off["mixb"]:_off["mixb"] + 12],
                                             scalar1=0.5, scalar2=None, op0=ALU.mult),
            reads=[("vecs",)], writes=[("par", "hb")])
        ADD("act", lambda e: e.activation(out=sc[:], in_=cT[:], func=AF.Silu),
              reads=[("cT",)], writes=[("sc",)])


        ada_issued = [0]

        def ada_stage(ci):
            slot = ci % 4
            sbuf_, skn = (act, "act") if slot < 2 else (u2, "u2")
            h0 = (slot % 2) * 4
            stage = sbuf_[:, h0:h0 + 4, :].rearrange("p a (k n) -> p (a k) n", k=2)
            skeys = [(skn, h0 + i_) for i_ in range(4)]
            return slot, stage, skeys

        def ada_dma_upto(cmax):
            if dry[0]:
                return
            while ada_issued[0] < min(cmax, 36):
                cj = ada_issued[0]
                slot_, stage_, skeys_ = ada_stage(cj)

                def _ld(e, stage_=stage_, cj=cj):
                    return e.dma_start(out=stage_.rearrange("p k n -> p (k n)"), in_=ada_d[cj])
                ADD("pool", _ld, writes=skeys_, dma=("ada", slot_))
                ada_issued[0] += 1

        def g_ada(c0, c1):
          for ci in range(c0, c1):
              ada_dma_upto(ci + 4)
              slot, stage, skeys = ada_stage(ci)
              for n in range(2):
                  tile_i = ci * 2 + n
                  j = tile_i // 8
                  bank = 6 + (j % 2)

                  def _mm(e, stage=stage, n=n, tile_i=tile_i, bank=bank):
                      ins = None
                      for kt in range(NFT):
                          ins = e.matmul(out=ps[bank][:, tile_i * NSEQ:(tile_i + 1) * NSEQ],
                                         lhsT=stage[:, kt, n * 128:(n + 1) * 128],
                                         rhs=sc[:, kt, :], start=(kt == 0), stop=(kt == NFT - 1))
                      return ins
                  ADD("pe", _mm, reads=skeys + [("sc",)],
                        writes=[("ps", bank)])
                  if tile_i % 8 == 7:
                      for s in range(NSEQ):
                          def _ev(e, j=j, s=s, bank=bank):
                              pv = ps[bank][:, j * 8 * NSEQ:(j + 1) * 8 * NSEQ].rearrange(
                                  "p (t s) -> p t s", s=NSEQ)[:, :, s]
                              return e.tensor_tensor(out=modv(j, s), in0=pv,
                                                     in1=vecs[:, _off["ada_b"] + j * 8:_off["ada_b"] + j * 8 + 8],
                                                     op=ALU.add)
                          ADD("dve", _ev, reads=[("ps", bank), ("vecs",)], writes=[("mod", j)])
                      k = j // 3
                      if j % 3 == 2:
                          coef = (1.0 if k == 1 else 0.5) / ALPHA
                          for s in range(3):
                              ADD("dve", lambda e, k=k, s=s: e.tensor_scalar(
                                  out=par[:, P_s1[k][s]:P_s1[k][s] + 8], in0=modv(3 * k + 1, s),
                                  scalar1=1.0, scalar2=None, op0=ALU.add),
                                  reads=[("mod", 3 * k + 1)], writes=[("par", k)])
                              ADD("dve", lambda e, k=k, s=s, coef=coef: e.tensor_scalar(
                                  out=par[:, P_gp[k][s]:P_gp[k][s] + 8], in0=modv(3 * k + 2, s),
                                  scalar1=coef, scalar2=None, op0=ALU.mult),
                                  reads=[("mod", 3 * k + 2)], writes=[("par", k)])
                              if k >= 1:
                                  lb = _off["ln_b"] + (k - 1) * 8
                                  ADD("dve", lambda e, k=k, s=s, lb=lb: e.tensor_tensor(
                                      out=par[:, P_tmp:P_tmp + 8], in0=vecs[:, lb:lb + 8],
                                      in1=par[:, P_s1[k][s]:P_s1[k][s] + 8], op=ALU.mult),
                                      reads=[("par", k), ("vecs",)], writes=[("ptmp",)])
                                  ADD("dve", lambda e, k=k, s=s: e.tensor_tensor(
                                      out=par[:, P_tB[k][s]:P_tB[k][s] + 8], in0=par[:, P_tmp:P_tmp + 8],
                                      in1=modv(3 * k, s), op=ALU.add),
                                      reads=[("ptmp",), ("mod", 3 * k)], writes=[("par", k)])
                              if k == 1:
                                  bo = _off["b_out"]
                                  lb0 = _off["ln_b"]
                                  ADD("dve", lambda e, s=s, bo=bo: e.tensor_tensor(
                                      out=par[:, P_tmp:P_tmp + 8], in0=par[:, P_gp[1][s]:P_gp[1][s] + 8],
                                      in1=vecs[:, bo:bo + 8], op=ALU.mult),
                                      reads=[("par", 1), ("vecs",)], writes=[("ptmp",)])
                                  ADD("dve", lambda e, s=s, lb0=lb0: e.tensor_tensor(
                                      out=par[:, P_xb0[s]:P_xb0[s] + 8], in0=par[:, P_tmp:P_tmp + 8],
                                      in1=vecs[:, lb0:lb0 + 8], op=ALU.add),
                                      reads=[("ptmp",), ("vecs",)], writes=[("par", 1)])
              yield 7.0

        def f_chunks(f):
            return [(w1_d[f, c], 4096) for c in range(11)] + [(wd_d[f, mo], DFF) for mo in range(8)]
        m_chunks = [(wmi_d[c], 4096) for c in range(3)] + [(wpo_d, 512)] + [(wo_d[c], 4096) for c in range(2)]
        NFC = 19
        NMC = 6
        f_order = []
        f_index = {}
        wst = {"F": 0, "M": 0, "ntiles": 0}

        def _prefetch(which, upto):
            if which == "F":
                total = len(f_order) * NFC
            else:
                total = wst["ntiles"] * NMC
            upto = min(upto, total - 1)
            while wst[which] <= upto:
                gi = wst[which]
                if which == "F":
                    tk = f_order[gi // NFC]
                    src, n = f_chunks(0 if tk[1] == 0 else 1)[gi % NFC]
                    slot = gi % NSF
                    ringt = ringF
                else:
                    src, n = m_chunks[gi % NMC]
                    slot = gi % NSM
                    ringt = ringM

                def _ld(e, src=src, n=n, slot=slot, ringt=ringt):
                    return e.dma_start(out=ringt[:, slot, 0:n], in_=src)
                ADD("pool", _ld, writes=[("ring" + which, slot)], dma=("ring" + which, slot))
                wst[which] += 1

        def wslotF(tidx, k, local):
            if dry[0]:
                return 0
            gi = f_index[(tidx, k)] * NFC + local
            _prefetch("F", gi + NSF - 1)
            return gi % NSF

        def wslotM(tidx, local):
            if dry[0]:
                return 0
            gi = tidx * NMC + local
            _prefetch("M", gi + NSM - 1)
            return gi % NSM

        CH_WMI = 0
        CH_WPO = 3
        CH_WO = 4

        def segs(cfg):
            return [(si, cfg.segseq[si], si * cfg.L, (si + 1) * cfg.L) for si in range(cfg.nseg)]

        x_issued = set()

        def issue_x(cfg):
            if dry[0] or cfg.tidx in x_issued:
                return
            x_issued.add(cfg.tidx)
            for bi, (c0, rows) in enumerate(cfg.blocks):
                def _ld(e, bi=bi, c0=c0, rows=rows, cfg=cfg):
                    return e.dma_start(out=xt[:rows, bi, :], in_=cfg.xsrc[c0:c0 + rows, :])
                ADD("sp", _ld, writes=[("xt", bi)], dma=("xin", bi))

        def load_in(cfg):
            T = cfg.T
            xfm = xfm2[cfg.par]
            XK = lambda ft_: ("xfm", cfg.par, ft_)
            issue_x(cfg)
            for ft in range(NFT):
                bank = 4 + (ft % 2)

                def _tr(e, ft=ft, bank=bank):
                    ins = None
                    for bi, (c0, rows) in enumerate(cfg.blocks):
                        ins = e.transpose(out=ps[bank][:, c0:c0 + rows],
                                          in_=xt[:rows, bi, ft * 128:(ft + 1) * 128],
                                          identity=ident[:rows, :rows])
                    return ins
                ADD("pe", _tr, reads=[("xt", bi) for bi in range(len(cfg.blocks))] + [("vecs",)],
                      writes=[("ps", bank)])
                ADD("act", lambda e, ft=ft, bank=bank: e.activation(out=xfm[:, ft, 0:T], in_=ps[bank][:, 0:T],
                                                                   func=AF.Copy),
                      reads=[("ps", bank)], writes=[XK(ft)])
                for (si, s, a, b) in segs(cfg):
                    ADD("act", lambda e, ft=ft, bank=bank, s=s, a=a, b=b: e.activation(
                        out=u[:, ft, a:b], in_=ps[bank][:, a:b], func=AF.Identity,
                        scale=pc(P_s1[0][s], ft), bias=modc(0, ft, s)),
                        reads=[("ps", bank), ("par", 0), ("mod", 0)], writes=[("u", ft)])
                yield 1.0 * T / 512
            if not dry[0] and cfg.tidx + 1 < len(tiles):
                issue_x(tiles[cfg.tidx + 1])

        def layer_norm(cfg, k, hook=None):
            T = cfg.T
            xfm = xfm2[cfg.par]
            XK = lambda ft_: ("xfm", cfg.par, ft_)
            ubuf, ukn = (act, "act") if k == 0 else (u2, "u2")
            if k == 1:
                yield ("need", ("f2in", cfg.tidx - 1))
            yield ("lock", "ln")
            for ft in range(NFT):
                r = ft % 2
                ADD("dve", lambda e, ft=ft, r=r: e.tensor_copy(out=rr[:, r, 0:T], in_=xfm[:, ft, 0:T]),
                      reads=[XK(ft)], writes=[("rr", r)])
                ADD("act", lambda e, ft=ft, r=r: e.activation(out=sq[:, r, 0:T], in_=xfm[:, ft, 0:T], func=AF.Square),
                      reads=[XK(ft)], writes=[("sq", r)])

                def _st(e, ft=ft, r=r):
                    e.matmul(out=ps[6][:, 0:T], lhsT=ones[:], rhs=rr[:, r, 0:T],
                             start=(ft == 0), stop=(ft == NFT - 1))
                    return e.matmul(out=ps[7][:, 0:T], lhsT=ones[:], rhs=sq[:, r, 0:T],
                                    start=(ft == 0), stop=(ft == NFT - 1))
                ADD("pe", _st, reads=[("rr", r), ("sq", r), ("ones",)], writes=[("ps", 6), ("ps", 7)])
                yield 0.6 * T / 512
            stats(cfg, 1.0 / D, 0)
            yield 6.0
            def opA(ft):
                ADD("dve", lambda e, ft=ft: e.tensor_tensor(out=xfm[:, ft, 0:T], in0=xfm[:, ft, 0:T],
                                                               in1=mean[:, 0:T], op=ALU.subtract),
                      reads=[XK(ft), ("mean",)], writes=[XK(ft)])

            def opB(ft):
                ADD("dve", lambda e, ft=ft: e.scalar_tensor_tensor(
                    out=xfm[:, ft, 0:T], in0=xfm[:, ft, 0:T], scalar=vcol("ln_g", k * 8 + ft),
                    in1=rstd[:, 0:T], op0=ALU.mult, op1=ALU.mult),
                    reads=[XK(ft), ("rstd",), ("vecs",)], writes=[XK(ft)])
                if k < 2:
                    for (si, s, a, b) in segs(cfg):
                        ADD("act", lambda e, ft=ft, s=s, a=a, b=b: e.activation(
                            out=ubuf[:, ft, a:b], in_=xfm[:, ft, a:b], func=AF.Identity,
                            scale=pc(P_s1[k + 1][s], ft), bias=pc(P_tB[k + 1][s], ft)),
                            reads=[XK(ft), ("par", k + 1)], writes=[(ukn, ft)])

            def opX(ft):
                if k == 0:
                    for (si, s, a, b) in segs(cfg):
                        ADD("dve", lambda e, ft=ft, s=s, a=a, b=b: e.tensor_scalar(
                            out=xfm[:, ft, a:b], in0=xfm[:, ft, a:b], scalar1=pc(P_xb0[s], ft),
                            scalar2=None, op0=ALU.add),
                            reads=[XK(ft), ("par", 1)], writes=[XK(ft)])
                else:
                    ADD("dve", lambda e, ft=ft: e.tensor_scalar(
                        out=xfm[:, ft, 0:T], in0=xfm[:, ft, 0:T], scalar1=vcol("ln_b", k * 8 + ft),
                        scalar2=None, op0=ALU.add),
                        reads=[XK(ft), ("vecs",)], writes=[XK(ft)])

            for ft in range(3):
                opA(ft)
            for ft in range(NFT):
                opB(ft)
                if ft + 3 < NFT:
                    opA(ft + 3)
                if ft >= 1:
                    opX(ft - 1)
                if ft == 4 and hook is not None:
                    yield from hook
                yield 1.7 * T / 512
            opX(NFT - 1)
            yield ("unlock", "ln")

        def stats(cfg, inv_n, eps_idx):
            T = cfg.T
            ADD("act", lambda e: e.activation(out=mean[:, 0:T], in_=ps[6][:, 0:T], func=AF.Identity, scale=inv_n),
                  reads=[("ps", 6)], writes=[("mean",)])
            ADD("dve", lambda e: e.tensor_tensor(out=sig[:, 1, 0:T], in0=mean[:, 0:T], in1=mean[:, 0:T], op=ALU.mult),
                  reads=[("mean",)], writes=[("sig", 1)])
            ADD("dve", lambda e: e.scalar_tensor_tensor(out=rstd[:, 0:T], in0=ps[7][:, 0:T], scalar=inv_n,
                                                          in1=sig[:, 1, 0:T], op0=ALU.mult, op1=ALU.subtract),
                  reads=[("ps", 7), ("sig", 1)], writes=[("rstd",)])
            ADD("act", lambda e: e.activation(out=rstd[:, 0:T], in_=rstd[:, 0:T], func=AF.Sqrt,
                                                bias=vcol("eps", eps_idx)),
                  reads=[("rstd",), ("vecs",)], writes=[("rstd",)])
            ADD("dve", lambda e: e.reciprocal(out=rstd[:, 0:T], in_=rstd[:, 0:T]),
                  reads=[("rstd",)], writes=[("rstd",)])

        def ffn(cfg, k):
            T = cfg.T
            f = 0 if k == 0 else 1
            xfm = xfm2[cfg.par]
            XK = lambda ft_: ("xfm", cfg.par, ft_)
            uin, ukn_ = (u, "u") if k == 0 else (u2, "u2")
            ukeys = [(ukn_, ft) for ft in range(NFT)]
            for j in range(NKF):
                c = j // 2
                slot = wslotF(cfg.tidx, k, c)
                wv = ringF[:, slot, :].rearrange("p (k m c) -> p k m c", k=8, m=4)
                bg = (j % 2) * 2
                bv = bg + 1
                for which, bank in ((0, bg), (1, bv)):
                    mt = (j % 2) * 2 + which

                    def _mm(e, wv=wv, mt=mt, bank=bank):
                        ins = None
                        for kt in range(NFT):
                            ins = e.matmul(out=ps[bank][:, 0:T], lhsT=wv[:, kt, mt, :], rhs=uin[:, kt, 0:T],
                                           start=(kt == 0), stop=(kt == NFT - 1))
                        return ins
                    ADD("pe", _mm, reads=ukeys + [("ringF", slot)], writes=[("ps", bank)])
                r = j % 2
                ADD("act", lambda e, r=r, bg=bg: e.activation(out=sg[:, r, 0:T], in_=ps[bg][:, 0:T], func=AF.Silu),
                      reads=[("ps", bg)], writes=[("sg", r)])
                ADD("dve", lambda e, r=r, bv=bv, j=j: e.tensor_tensor(out=h[:, j, 0:T], in0=sg[:, r, 0:T],
                                                                         in1=ps[bv][:, 0:T], op=ALU.mult),
                      reads=[("sg", r), ("ps", bv)], writes=[("h", j)])
                yield 3.9 * (0.5 + 0.5 * T / 512)
            if k == 2:
                yield ("mark", ("f2in", cfg.tidx))
            for mo in range(NFT):
                slot = wslotF(cfg.tidx, k, 11 + mo)
                wv = ringF[:, slot, 0:DFF].rearrange("p (k c) -> p k c", k=NKF)
                bank = 4 + (mo % 2)
                for half in range(2):
                    k0, k1 = (0, 11) if half == 0 else (11, 22)

                    def _mm(e, wv=wv, bank=bank, k0=k0, k1=k1):
                        ins = None
                        for kf in range(k0, k1):
                            ins = e.matmul(out=ps[bank][:, 0:T], lhsT=wv[:, kf, :], rhs=h[:, kf, 0:T],
                                           start=(kf == 0), stop=(kf == NKF - 1))
                        return ins
                    ADD("pe", _mm, reads=[("h", kf) for kf in range(k0, k1)] + [("ringF", slot)],
                          writes=[("ps", bank)])
                for (si, s, a, b) in segs(cfg):
                    ADD("dve", lambda e, mo=mo, bank=bank, s=s, a=a, b=b: e.scalar_tensor_tensor(
                        out=xfm[:, mo, a:b], in0=ps[bank][:, a:b], scalar=pc(P_gp[k][s], mo),
                        in1=xfm[:, mo, a:b], op0=ALU.mult, op1=ALU.add),
                        reads=[("ps", bank), XK(mo), ("par", k)], writes=[XK(mo)])
                yield 5.4 * (0.5 + 0.5 * T / 512)

        def seg_view(ap2d, nseg, w, lo, hi):
            return ap2d[:, 0:nseg * w].rearrange("p (s w) -> p s w", s=nseg)[:, :, lo:hi]

        def mixer(cfg):
            T, nseg, L = cfg.T, cfg.nseg, cfg.L
            xfm = xfm2[cfg.par]
            XK = lambda ft_: ("xfm", cfg.par, ft_)
            CWD = CH + L
            PWD = PH + L
            allcin = [("cin", i) for i in range(4)]
            allpin = [("pin", i) for i in range(4)]
            if cfg.kind == "S":
                for i in range(4):
                    ADD("sp", lambda e, i=i: e.dma_start(
                        out=cin[:, i, 0:2 * CWD].rearrange("p (s w) -> p s w", s=2)[:, :, 0:CH],
                        in_=cc_d[:, i * 2 * CH:(i + 1) * 2 * CH].rearrange("p (s t) -> p s t", s=2)),
                        writes=[("cin", i)], dma=("d_cc", i))
                    ADD("sp", lambda e, i=i: e.dma_start(
                        out=pin[:, i, 0:2 * PWD].rearrange("p (s w) -> p s w", s=2)[:, :, 0:PH],
                        in_=pc_d[:, i * 2 * PH:(i + 1) * 2 * PH].rearrange("p (s t) -> p s t", s=2)),
                        writes=[("pin", i)], dma=("d_pc", i))
                ADD("pool", lambda e: e.tensor_copy(out=cin[:, :, 2 * CWD:2 * CWD + CH], in_=cin[:, :, 512:512 + CH]),
                      reads=allcin, writes=allcin)
                ADD("pool", lambda e: e.tensor_copy(out=pin[:, :, 2 * PWD:2 * PWD + PH], in_=pin[:, :, 512:512 + PH]),
                      reads=allpin, writes=allpin)
            elif cfg.kind == "P0":
                ADD("pool", lambda e: e.memset(cin[:, :, 0:CH], 0.0), writes=allcin)
                ADD("pool", lambda e: e.memset(pin[:, :, 0:PH], 0.0), writes=allpin)
            else:
                ADD("pool", lambda e: e.tensor_copy(out=cin[:, :, 0:CH], in_=cin[:, :, 512:512 + CH]),
                      reads=allcin, writes=allcin)
                ADD("pool", lambda e: e.tensor_copy(out=pin[:, :, 0:PH], in_=pin[:, :, 512:512 + PH]),
                      reads=allpin, writes=allpin)
            ukeys = [("act", ft) for ft in range(NFT)]

            def inproj(local_chunk, mt, bank):
                slot = wslotM(cfg.tidx, CH_WMI + local_chunk)
                wv = ringM[:, slot, :].rearrange("p (k m c) -> p k m c", k=8, m=4)

                def _mm(e, wv=wv, mt=mt, bank=bank):
                    ins = None
                    for kt in range(NFT):
                        ins = e.matmul(out=ps[bank][:, 0:T], lhsT=wv[:, kt, mt, :], rhs=act[:, kt, 0:T],
                                       start=(kt == 0), stop=(kt == NFT - 1))
                    return ins
                ADD("pe", _mm, reads=ukeys + [("ringM", slot)], writes=[("ps", bank)])

            for i in range(4):
                c = i // 2
                bb = (i % 2) * 2
                ba = bb + 1
                inproj(c, (i % 2) * 2, bb)
                inproj(c, (i % 2) * 2 + 1, ba)
                r = 0
                mb = _off["mixb"] + c * 4 + (i % 2) * 2
                hbc = P_hb + c * 4 + (i % 2) * 2
                ADD("act", lambda e, r=r, bb=bb, hbc=hbc: e.activation(
                    out=sig[:, r, 0:T], in_=ps[bb][:, 0:T], func=AF.Tanh, scale=0.5, bias=par[:, hbc:hbc + 1]),
                    reads=[("ps", bb), ("par", "hb")], writes=[("sig", r)])
                ADD("dve", lambda e, r=r: e.tensor_scalar(
                    out=sig[:, r, 0:T], in0=sig[:, r, 0:T], scalar1=1.0, scalar2=0.5, op0=ALU.add, op1=ALU.mult),
                    reads=[("sig", r)], writes=[("sig", r)])
                ADD("dve", lambda e, i=i, r=r, ba=ba, mb=mb: e.scalar_tensor_tensor(
                    out=seg_view(cin[:, i, :], nseg, CWD, CH, CWD),
                    in0=ps[ba][:, 0:T].rearrange("p (s l) -> p s l", s=nseg),
                    scalar=vecs[:, mb + 1:mb + 2],
                    in1=sig[:, r, 0:T].rearrange("p (s l) -> p s l", s=nseg),
                    op0=ALU.add, op1=ALU.mult),
                    reads=[("ps", ba), ("sig", r), ("vecs",)], writes=[("cin", i)])
                yield 4.0 * (0.5 + 0.5 * T / 512)
            for i in range(4):
                bank = i % 4
                inproj(2, i, bank)
                mb = _off["mixb"] + 8 + i
                ADD("act", lambda e, i=i, bank=bank, mb=mb: e.activation(
                    out=seg_view(pin[:, i, :], nseg, PWD, PH, PWD),
                    in_=ps[bank][:, 0:T].rearrange("p (s l) -> p s l", s=nseg),
                    func=AF.Identity, bias=vecs[:, mb:mb + 1]),
                    reads=[("ps", bank), ("vecs",)], writes=[("pin", i)])
                yield 2.0 * (0.5 + 0.5 * T / 512)
            if cfg.kind == "P0":
                ADD("dve", lambda e: e.tensor_scalar(out=cin[:, :, CH:CH + 64], in0=cin[:, :, CH:CH + 64],
                                                       scalar1=vcol("mask", 0), scalar2=None, op0=ALU.mult),
                      reads=allcin + [("vecs",)], writes=allcin)
                ADD("dve", lambda e: e.tensor_scalar(out=pin[:, :, PH:PH + 64], in0=pin[:, :, PH:PH + 64],
                                                       scalar1=vcol("mask", 0), scalar2=None, op0=ALU.mult),
                      reads=allpin + [("vecs",)], writes=allpin)
            if cfg.kind == "S":
                emit_state(cfg, 0, scs_d[0], sps_d[0])
                emit_state(cfg, 1, scs_d[1], sps_d[1])
                emit_state(cfg, 2, scp_d, spp_d)
            W = PWD
            for g in range(4):
                pv = lambda lo, hi, g=g: seg_view(pin[:, g, :], nseg, W, lo, hi)
                av_ = lambda lo, hi: seg_view(sA[:], nseg, W, lo, hi)
                bv_ = lambda lo, hi: seg_view(sB[:], nseg, W, lo, hi)
                ADD("pool", lambda e, pv=pv, av_=av_: e.tensor_tensor(
                    out=av_(1, W), in0=pv(1, W), in1=pv(0, W - 1), op=ALU.add),
                    reads=[("pin", g)], writes=[("sA",)])
                last = av_
                lastkey = ("sA",)
                if g >= 1:
                    ADD("pool", lambda e, av_=av_, bv_=bv_: e.tensor_tensor(
                        out=bv_(3, W), in0=av_(3, W), in1=av_(1, W - 2), op=ALU.add),
                        reads=[("sA",)], writes=[("sB",)])
                    last, lastkey = bv_, ("sB",)
                if g >= 2:
                    ADD("pool", lambda e, av_=av_, bv_=bv_: e.tensor_tensor(
                        out=av_(7, W), in0=bv_(7, W), in1=bv_(3, W - 4), op=ALU.add),
                        reads=[("sB",)], writes=[("sA",)])
                    last, lastkey = av_, ("sA",)
                if g >= 3:
                    ADD("pool", lambda e, av_=av_, bv_=bv_: e.tensor_tensor(
                        out=bv_(15, W), in0=av_(15, W), in1=av_(7, W - 8), op=ALU.add),
                        reads=[("sA",)], writes=[("sB",)])
                    last, lastkey = bv_, ("sB",)
                wnd = float(2 ** (g + 1))
                ADD("dve", lambda e, g=g, last=last, pv=pv, wnd=wnd: e.scalar_tensor_tensor(
                    out=pooled[:, g, 0:T].rearrange("p (s l) -> p s l", s=nseg),
                    in0=last(PH, W), scalar=1.0 / wnd, in1=pv(PH, W), op0=ALU.mult, op1=ALU.subtract),
                    reads=[lastkey, ("pin", g)], writes=[("pooled", g)])
                if cfg.kind == "P0":
                    ic = _off["invcnt"] + g * 16
                    lastflat = sA if lastkey == ("sA",) else sB
                    ADD("pool", lambda e, lastflat=lastflat, ic=ic: e.tensor_tensor(
                        out=t16[:], in0=lastflat[:, PH + 64:PH + 80], in1=vecs[:, ic:ic + 16], op=ALU.mult),
                        reads=[lastkey, ("vecs",)], writes=[("t16",)])
                    ADD("pool", lambda e, g=g: e.tensor_tensor(
                        out=pooled[:, g, 64:80], in0=t16[:], in1=pin[:, g, PH + 64:PH + 80], op=ALU.subtract),
                        reads=[("t16",), ("pin", g), ("pooled", g)], writes=[("pooled", g)])
                slot = wslotM(cfg.tidx, CH_WPO)
                wv = ringM[:, slot, 0:512].rearrange("p (g d) -> p g d", g=4)
                bank = 4 + (g % 2)
                ADD("pe", lambda e, g=g, wv=wv, bank=bank: e.matmul(
                    out=ps[bank][:, 0:T], lhsT=wv[:, g, :], rhs=pooled[:, g, 0:T], start=True, stop=True),
                    reads=[("pooled", g), ("ringM", slot)], writes=[("ps", bank)])
                ADD("dve", lambda e, g=g, bank=bank: e.tensor_scalar(
                    out=act[:, 4 + g, 0:T], in0=ps[bank][:, 0:T], scalar1=vcol("pool_b", g),
                    scalar2=vcol("pool_s", g), op0=ALU.add, op1=ALU.mult),
                    reads=[("ps", bank), ("vecs",)], writes=[("act", 4 + g)])
                yield 1.5 * T / 512
            cvs = [(lambda k, i=i: seg_view(cin[:, i, :], nseg, CWD, k, k + L)) for i in range(4)]
            avs = [acc[:, i, 0:T].rearrange("p (s l) -> p s l", s=nseg) for i in range(4)]
            for i in range(4):
                ADD("dve", lambda e, i=i: e.tensor_scalar(
                    out=avs[i], in0=cvs[i](0), scalar1=vcol("conv_w", 0 * 4 + i), scalar2=vcol("conv_b", i),
                    op0=ALU.mult, op1=ALU.add),
                    reads=[("cin", i), ("vecs",)], writes=[("acc", i)])
                yield 0.4 * T / 512
            for k in range(1, CW):
                for i in range(4):
                    ADD("dve", lambda e, i=i, k=k: e.scalar_tensor_tensor(
                        out=avs[i], in0=cvs[i](k), scalar=vcol("conv_w", k * 4 + i), in1=avs[i],
                        op0=ALU.mult, op1=ALU.add),
                        reads=[("cin", i), ("acc", i), ("vecs",)], writes=[("acc", i)])
                    yield 0.62 * T / 512
            yield ("lock", "ln")
            for i in range(4):
                ADD("dve", lambda e, i=i: e.tensor_copy(out=rr[:, i % 2, 0:T], in_=acc[:, i, 0:T]),
                      reads=[("acc", i)], writes=[("rr", i % 2)])
                ADD("act", lambda e, i=i: e.activation(out=sq[:, i % 2, 0:T], in_=acc[:, i, 0:T], func=AF.Square),
                      reads=[("acc", i)], writes=[("sq", i % 2)])

                def _st(e, i=i):
                    e.matmul(out=ps[6][:, 0:T], lhsT=ones[:], rhs=rr[:, i % 2, 0:T],
                             start=(i == 0), stop=(i == 3))
                    return e.matmul(out=ps[7][:, 0:T], lhsT=ones[:], rhs=sq[:, i % 2, 0:T],
                                    start=(i == 0), stop=(i == 3))
                ADD("pe", _st, reads=[("rr", i % 2), ("sq", i % 2), ("ones",)], writes=[("ps", 6), ("ps", 7)])
                yield 0.6 * T / 512
            stats(cfg, 1.0 / 512, 1)
            yield 6.0
            for i in range(4):
                ADD("dve", lambda e, i=i: e.tensor_tensor(out=acc[:, i, 0:T], in0=acc[:, i, 0:T],
                                                             in1=mean[:, 0:T], op=ALU.subtract),
                      reads=[("acc", i), ("mean",)], writes=[("acc", i)])
                ADD("dve", lambda e, i=i: e.scalar_tensor_tensor(
                    out=acc[:, i, 0:T], in0=acc[:, i, 0:T], scalar=vcol("cln_g", i), in1=rstd[:, 0:T],
                    op0=ALU.mult, op1=ALU.mult),
                    reads=[("acc", i), ("rstd",), ("vecs",)], writes=[("acc", i)])
                ADD("act", lambda e, i=i: e.activation(out=act[:, i, 0:T], in_=acc[:, i, 0:T], func=AF.Silu,
                                                         bias=vcol("cln_b", i)),
                      reads=[("acc", i), ("vecs",)], writes=[("act", i)])
                yield 1.9 * T / 512
            yield ("unlock", "ln")
            akeys = [("act", kk) for kk in range(8)]
            for mo in range(NFT):
                slot = wslotM(cfg.tidx, CH_WO + mo // 4)
                wv = ringM[:, slot, :].rearrange("p (m k c) -> p m k c", m=4, k=8)
                bank = 4 + (mo % 2)

                def _mm(e, wv=wv, mo=mo, bank=bank):
                    ins = None
                    for kk in range(8):
                        ins = e.matmul(out=ps[bank][:, 0:T], lhsT=wv[:, mo % 4, kk, :], rhs=act[:, kk, 0:T],
                                       start=(kk == 0), stop=(kk == 7))
                    return ins
                ADD("pe", _mm, reads=akeys + [("ringM", slot)], writes=[("ps", bank)])
                for (si, s, a, b) in segs(cfg):
                    ADD("dve", lambda e, mo=mo, bank=bank, s=s, a=a, b=b: e.scalar_tensor_tensor(
                        out=xfm[:, mo, a:b], in0=ps[bank][:, a:b], scalar=pc(P_gp[1][s], mo),
                        in1=xfm[:, mo, a:b], op0=ALU.mult, op1=ALU.add),
                        reads=[("ps", bank), XK(mo), ("par", 1)], writes=[XK(mo)])
                yield 2.0 * (0.5 + 0.5 * T / 512)

        st_count = [0]

        def emit_state(cfg, seg, conv_dst, pool_dst):
            L = cfg.L
            CWD = CH + L
            PWD = PH + L
            for (buf, key, wd, hist, dst, dkey) in ((cin, "cin", CWD, CH, conv_dst, "stc"),
                                                     (pin, "pin", PWD, PH, pool_dst, "stp")):
                r = st_count[0] % 2
                st_count[0] += 1
                bank = 4 + r
                c0 = seg * wd + L

                def _tr(e, buf=buf, bank=bank, c0=c0, hist=hist):
                    ins = None
                    for i in range(4):
                        ins = e.transpose(out=ps[bank][:hist, i * 128:(i + 1) * 128],
                                          in_=buf[:, i, c0:c0 + hist], identity=ident)
                    return ins
                ADD("pe", _tr, reads=[(key, i) for i in range(4)] + [("vecs",)], writes=[("ps", bank)])
                ADD("act", lambda e, bank=bank, r=r, hist=hist: e.activation(
                    out=yt[:hist, r, 0:512], in_=ps[bank][:hist, :], func=AF.Copy),
                    reads=[("ps", bank)], writes=[("yt", r, 0)])
                ADD("sp", lambda e, r=r, hist=hist, dst=dst: e.dma_start(out=dst, in_=yt[:hist, r, 0:512]),
                      reads=[("yt", r, 0)], dma=("st", r))

        def store_out(cfg, halves=(0, 1)):
            xfm = xfm2[cfg.par]
            XK = lambda ft_: ("xfm", cfg.par, ft_)
            for half in halves:
                for (bi, dst) in cfg.outblocks:
                    c0, rows = cfg.blocks[bi]
                    r = bi % 2
                    bank = 4 + (bi % 2)

                    def _tr(e, half=half, bank=bank, c0=c0, rows=rows):
                        ins = None
                        for q in range(4):
                            ft = half * 4 + q
                            ins = e.transpose(out=ps[bank][:rows, q * 128:(q + 1) * 128],
                                              in_=xfm[:, ft, c0:c0 + rows], identity=ident)
                        return ins
                    ADD("pe", _tr, reads=[XK(half * 4 + q) for q in range(4)] + [("vecs",)],
                          writes=[("ps", bank)])
                    ADD("act", lambda e, half=half, bank=bank, rows=rows, r=r: e.activation(
                        out=yt[:rows, r, half * 512:(half + 1) * 512], in_=ps[bank][:rows, :], func=AF.Copy),
                        reads=[("ps", bank)], writes=[("yt", r, half)])
                    ADD("sp", lambda e, rows=rows, r=r, dst=dst, half=half: e.dma_start(
                        out=dst[:, half * 512:(half + 1) * 512], in_=yt[:rows, r, half * 512:(half + 1) * 512]),
                        reads=[("yt", r, half)], dma=("yout", r, half))
                    yield 1.0

        tiles = []
        tidx = 0
        for n in range(n_ptiles):
            kind = "P0" if n == 0 else "P"
            xsrc = xp_d[n * 512:(n + 1) * 512, :]
            outb = [(bi, yp_d[n * 512 + bi * 128:n * 512 + (bi + 1) * 128, :]) for bi in range(4)]
            tiles.append(TileCfg(f"P{n}", 512, 1, 512, [0], [(bi * 128, 128) for bi in range(4)], xsrc,
                                 outb, kind, tidx))
            tidx += 1
        if do_s:
            tiles.append(TileCfg("S", 192, 3, 64, [1, 2, 0], [(0, 128), (128, 64)], xs_d,
                                 [(0, ys_d[0:128, :]), (1, ys_d[128:192, :])], "S", tidx))
            tidx += 1
        N = len(tiles)
        for t_ in tiles:
            t_.par = t_.tidx % 2
        f_order.append((0, 0))
        for n in range(N):
            if n >= 1:
                f_order.append((n - 1, 2))
            if n + 1 < N:
                f_order.append((n + 1, 0))
        f_order.append((N - 1, 2))
        for i_, tk in enumerate(f_order):
            f_index[tk] = i_
        wst["ntiles"] = N

        def chain(*gs):
            for g in gs:
                yield from g

        marks = set([("f2in", -1)])

        def run(g):
            for c in g:
                if isinstance(c, tuple) and c[0] == "mark" and not dry[0]:
                    marks.add(c[1])

        def total(mk):
            dry[0] = True
            t = sum(c for c in mk() if not isinstance(c, tuple))
            dry[0] = False
            return max(t, 1e-6)

        def merge(mka, mkb, bbias=1.0):
            Ta, Tb = total(mka), total(mkb)
            gens = {"a": mka(), "b": mkb()}
            t = {"a": 0.0, "b": 0.0}
            T_ = {"a": Ta, "b": Tb * bbias}
            done = {"a": False, "b": False}
            need = {"a": None, "b": None}
            wantlock = {"a": False, "b": False}
            holder = [None]

            def blocked(sd):
                if need[sd] is not None and need[sd] not in marks:
                    return True
                if wantlock[sd]:
                    if holder[0] is None:
                        holder[0] = sd
                        wantlock[sd] = False
                        return False
                    return holder[0] != sd
                return False

            while not (done["a"] and done["b"]):
                cands = [sd for sd in ("a", "b") if not done[sd] and not blocked(sd)]
                if not cands:
                    raise RuntimeError("merge deadlock")
                if len(cands) == 2 and holder[0] == "a":
                    sd = "a"
                elif len(cands) == 2:
                    sd = "a" if t["a"] / T_["a"] <= t["b"] / T_["b"] else "b"
                else:
                    sd = cands[0]
                need[sd] = None
                try:
                    c = next(gens[sd])
                except StopIteration:
                    done[sd] = True
                    if holder[0] == sd:
                        holder[0] = None
                    continue
                if isinstance(c, tuple):
                    if c[0] == "mark":
                        marks.add(c[1])
                    elif c[0] == "need":
                        need[sd] = c[1]
                    elif c[0] == "lock":
                        if holder[0] is None or holder[0] == sd:
                            holder[0] = sd
                        else:
                            wantlock[sd] = True
                    elif c[0] == "unlock":
                        if holder[0] == sd:
                            holder[0] = None
                else:
                    t[sd] += c

        run(g_ada(0, 12))
        t0 = tiles[0]
        merge(lambda: chain(load_in(t0), ffn(t0, 0)), lambda: g_ada(12, 36))
        def tail(t_):
            return chain(ffn(t_, 2), layer_norm(t_, 2, hook=store_out(t_, halves=(0,))), store_out(t_, halves=(1,)))

        def head(t_):
            return chain(load_in(t_), ffn(t_, 0))

        for n in range(N):
            tn = tiles[n]
            mkB = lambda tn=tn: chain(layer_norm(tn, 0), mixer(tn), layer_norm(tn, 1))
            parts = []
            if n >= 1:
                parts.append(lambda n=n: tail(tiles[n - 1]))
            if n + 1 < N:
                parts.append(lambda n=n: head(tiles[n + 1]))
            mkA = lambda parts=parts: chain(*[p() for p in parts])
            if parts and interleave:
                merge(mkA, mkB, bbias=1.04)
            else:
                run(mkA())
                run(mkB())
        run(tail(tiles[N - 1]))

        S.finalize()
        out_keys = [k for k in S.dma_counts if isinstance(k, tuple) and k[0] in ("yout", "st")]
        sem_es = contextlib.ExitStack()
        with sem_es:
            eng_sem = {e: sem_es.enter_context(nc.semaphore(f"sem_{e}")) for e in Sched.ENG}
            dma_sem = {}
            for i, key in enumerate(S.dma_counts):
                dma_sem[key] = sem_es.enter_context(nc.semaphore(f"dsem_{i}"))
            with nc.Block() as block:
                @block.sync
                def _(e):
                    S.emit_engine("sp", e, eng_sem, dma_sem, final_waits=out_keys)

                @block.tensor
                def _(e):
                    S.emit_engine("pe", e, eng_sem, dma_sem)

                @block.scalar
                def _(e):
                    S.emit_engine("act", e, eng_sem, dma_sem)

                @block.vector
                def _(e):
                    S.emit_engine("dve", e, eng_sem, dma_sem)

                @block.gpsimd
                def _(e):
                    S.emit_engine("pool", e, eng_sem, dma_sem)
    return nc, S


def _vec_tiles(v, nt):
    return np.ascontiguousarray(np.asarray(v, np.float32).reshape(nt, 128).T)


def prep_weights(inp):
    f32 = np.float32
    w_in = np.asarray(inp["ffn_w_in"], f32)[0]
    w1 = np.empty((2, 11, 128, 4096), f32)
    ct_order = []
    for c in range(11):
        for mt in range(4):
            j = 2 * c + mt // 2
            ct_order.append(j if mt % 2 == 0 else 22 + j)
    for f in range(2):
        W = w_in[f].reshape(8, 128, 44, 128)[:, :, ct_order, :]
        W = W.reshape(8, 128, 11, 4, 128).transpose(2, 1, 0, 3, 4)
        w1[f] = W.reshape(11, 128, 4096)
    w_dn = np.asarray(inp["ffn_w_down"], f32)[0]
    wd = np.empty((2, 8, 128, DFF), f32)
    for f in range(2):
        W = w_dn[f].reshape(22, 128, 8, 128).transpose(2, 1, 0, 3)
        wd[f] = W.reshape(8, 128, DFF)
    wmi_src = np.asarray(inp["mix_w_in"], f32)[0]
    W = wmi_src.reshape(8, 128, 12, 128)[:, :, MIX_CT, :]
    W = W.reshape(8, 128, 3, 4, 128).transpose(2, 1, 0, 3, 4)
    wmi = np.ascontiguousarray(W.reshape(3, 128, 4096))
    wpo = np.ascontiguousarray(np.asarray(inp["pool_w"], f32)[0].transpose(1, 0, 2).reshape(128, 512))
    wo_src = np.asarray(inp["mix_w_out"], f32)[0]
    W = wo_src.reshape(8, 128, 8, 128).transpose(2, 1, 0, 3)
    W = W.reshape(2, 4, 128, 8, 128).transpose(0, 2, 1, 3, 4)
    wo = np.ascontiguousarray(W.reshape(2, 128, 4096))
    ada_src = np.asarray(inp["ada_w"], f32)[0]
    ada = np.ascontiguousarray(ada_src.reshape(8, 128, 36, 256).transpose(2, 1, 0, 3).reshape(36, 128, 2048))
    return dict(w1=w1, wd=wd, wmi=wmi, wpo=wpo, wo=wo, ada_w=ada)


def prep_core(inp, i, shared_cols, npt=8):
    f32 = np.float32
    b, q = i // 4, i % 4
    xpr = np.asarray(inp["x_prompt"], f32)
    xsm = np.asarray(inp["x_sample"], f32)
    nmain = 512 * npt - 64
    if q > 0:
        halo = xpr[b, q * PCH - 64:q * PCH]
    else:
        halo = np.zeros((64, D), f32)
    xp = np.zeros((PCH, D), f32)
    xp[0:64] = halo
    xp[64:64 + nmain] = xpr[b, q * PCH:q * PCH + nmain]
    xs = np.ascontiguousarray(np.concatenate(
        [xsm[2 * i], xsm[2 * i + 1], xpr[b, q * PCH + nmain:q * PCH + nmain + 64]], axis=0))
    c3 = np.stack([np.asarray(inp["c_prompt"], f32)[b], np.asarray(inp["c_sample"], f32)[2 * i],
                   np.asarray(inp["c_sample"], f32)[2 * i + 1], np.asarray(inp["c_prompt"], f32)[b]], axis=0)
    cT = np.ascontiguousarray(c3.T.reshape(8, 128, NSEQ).transpose(1, 0, 2).reshape(128, 8 * NSEQ))
    cc = np.asarray(inp["cache_conv"], f32)[0, 2 * i:2 * i + 2]
    ccache = np.ascontiguousarray(cc.reshape(2, CH, 4, 128).transpose(3, 2, 0, 1).reshape(128, 4 * 2 * CH))
    pcs = np.asarray(inp["cache_pool"], f32)[0, 2 * i:2 * i + 2]
    pcache = np.ascontiguousarray(pcs.reshape(2, PH, 4, 128).transpose(3, 2, 0, 1).reshape(128, 4 * 2 * PH))
    vecs = shared_cols.copy()
    vecs[:, _off["mask"]] = 0.0 if q == 0 else 1.0
    inv = np.empty((4, 16), f32)
    for g in range(4):
        w = 2 ** (g + 1)
        for t in range(16):
            inv[g, t] = 1.0 / (min(t + 1, w) if q == 0 else w)
    vecs[:, _off["invcnt"]:_off["invcnt"] + 64] = inv.reshape(1, 64)
    return dict(xs=xs, xp=xp, cT=cT, ccache=ccache, pcache=pcache, vecs=vecs)


def shared_vecs(inp):
    f32 = np.float32
    v = np.zeros((128, NV), f32)
    v[:, _off["ident"]:_off["ident"] + 128] = np.eye(128, dtype=f32)
    v[:, _off["ada_b"]:_off["ada_b"] + 72] = _vec_tiles(np.asarray(inp["ada_b"])[0], 72)
    v[:, _off["ln_g"]:_off["ln_g"] + 24] = _vec_tiles(np.asarray(inp["ln_g"])[0].reshape(-1), 24)
    v[:, _off["ln_b"]:_off["ln_b"] + 24] = _vec_tiles(np.asarray(inp["ln_b"])[0].reshape(-1), 24)
    mb = _vec_tiles(np.asarray(inp["mix_b_in"])[0], 12)[:, MIX_CT]
    v[:, _off["mixb"]:_off["mixb"] + 12] = mb
    cw = np.asarray(inp["conv_w"], f32)[0]
    v[:, _off["conv_w"]:_off["conv_w"] + 124] = cw.reshape(31, 4, 128).transpose(2, 0, 1).reshape(128, 124)
    v[:, _off["conv_b"]:_off["conv_b"] + 4] = _vec_tiles(np.asarray(inp["conv_b"])[0], 4)
    v[:, _off["cln_g"]:_off["cln_g"] + 4] = _vec_tiles(np.asarray(inp["conv_ln_g"])[0], 4)
    v[:, _off["cln_b"]:_off["cln_b"] + 4] = _vec_tiles(np.asarray(inp["conv_ln_b"])[0], 4)
    v[:, _off["pool_b"]:_off["pool_b"] + 4] = _vec_tiles(np.asarray(inp["pool_b"])[0].reshape(-1), 4)
    v[:, _off["pool_s"]:_off["pool_s"] + 4] = _vec_tiles(np.asarray(inp["pool_scale"])[0], 4)
    v[:, _off["b_out"]:_off["b_out"] + 8] = _vec_tiles(np.asarray(inp["mix_b_out"])[0], 8)
    v[:, _off["eps"]] = EPS / (ALPHA * ALPHA)
    v[:, _off["eps"] + 1] = EPS
    return v


_NC_CACHE = {}


def kernel(**inputs):
    if "nc" not in _NC_CACHE:
        _NC_CACHE["nc"] = build_program()[0]
    nc = _NC_CACHE["nc"]
    wts = prep_weights(inputs)
    sv = shared_vecs(inputs)
    in_maps = []
    for i in range(NCORES):
        m = prep_core(inputs, i, sv)
        m.update(wts)
        in_maps.append(m)
    res = run_bass_kernel_spmd(nc, in_maps, core_ids=list(range(NCORES)))
    r = res.results
    f32 = np.float32
    y_prompt = np.empty((2, 16384, D), f32)
    y_sample = np.empty((16, 64, D), f32)
    scp = np.empty((1, 2, CH, 512), f32)
    spp = np.empty((1, 2, PH, 512), f32)
    scs = np.empty((1, 16, CH, 512), f32)
    sps = np.empty((1, 16, PH, 512), f32)
    for i in range(NCORES):
        b, q = i // 4, i % 4
        y_prompt[b, q * PCH:(q + 1) * PCH - 64] = r[i]["yp"][64:PCH]
        y_prompt[b, (q + 1) * PCH - 64:(q + 1) * PCH] = r[i]["ys"][128:192]
        y_sample[2 * i] = r[i]["ys"][0:64]
        y_sample[2 * i + 1] = r[i]["ys"][64:128]
        scs[0, 2 * i:2 * i + 2] = r[i]["scs"]
        sps[0, 2 * i:2 * i + 2] = r[i]["sps"]
        if q == 3:
            scp[0, b] = r[i]["scp"]
            spp[0, b] = r[i]["spp"]
    return (y_prompt, y_sample, scp, spp, scs, sps)
```

```python
import contextlib
import numpy as np
import concourse.bass as bass
import concourse.mybir as mybir
from concourse.bass_utils import run_bass_kernel_spmd

F32 = mybir.dt.float32
BF16 = mybir.dt.bfloat16
F32R = mybir.dt.float32r
AF = mybir.ActivationFunctionType
ALU = mybir.AluOpType

D = 1024
DFF = 2816
NFT = 8
NKF = 22
CW = 31
CH = 30
PH = 15
NCORES = 8
PCH = 4096
ALPHA = 2.0 ** 0.25
EPS = 1e-5
NSF = 3
NSM = 2
SLOT_ELEMS = 4096
NSEQ = 4

_off = {}
_o = 0
for _n, _w in [("ident", 128), ("ada_b", 72), ("ln_g", 24), ("ln_b", 24), ("mixb", 12),
               ("conv_w", 124), ("conv_b", 4), ("cln_g", 4), ("cln_b", 4), ("pool_b", 4),
               ("pool_s", 4), ("b_out", 8), ("mask", 1), ("invcnt", 64), ("eps", 2)]:
    _off[_n] = _o
    _o += _w
NV = _o

MIX_CT = [4, 0, 5, 1, 6, 2, 7, 3, 8, 9, 10, 11]


class _Op:
    __slots__ = ("eng", "fn", "deps", "pos", "signal", "count", "dma_key", "dma_val", "is_dma")


class Sched:
    ENG = ("pe", "act", "dve", "pool", "sp")

    def __init__(self):
        self.ops = []
        self.lastw = {}
        self.readers = {}
        self.eng_ops = {e: [] for e in self.ENG}
        self.dma_counts = {}

    def add(self, eng, fn, reads=(), writes=(), dma=None):
        op = _Op()
        op.eng = eng
        op.fn = fn
        op.is_dma = dma is not None
        op.signal = False
        op.count = 0
        deps = set()
        writes = list(writes) + [k for k in reads if k[0] == "ps"]
        reads = [k for k in reads if k[0] != "ps"]
        for k in reads:
            w = self.lastw.get(k)
            if w is not None:
                deps.add(w)
        for k in writes:
            w = self.lastw.get(k)
            if w is not None:
                deps.add(w)
            for r in self.readers.get(k, ()):
                deps.add(r)
        for k in reads:
            self.readers.setdefault(k, []).append(op)
        for k in writes:
            self.lastw[k] = op
            self.readers[k] = []
        op.deps = deps
        op.pos = len(self.eng_ops[eng])
        self.eng_ops[eng].append(op)
        if dma is not None:
            n = self.dma_counts.get(dma, 0) + 1
            self.dma_counts[dma] = n
            op.dma_key = dma
            op.dma_val = 16 * n
        else:
            op.dma_key = None
            op.dma_val = 0
        self.ops.append(op)
        return op

    @staticmethod
    def _needs(op, d):
        if d.is_dma:
            return True
        if d.eng != op.eng:
            return True
        if op.is_dma:
            return True
        if op.eng == "pe":
            return False
        return (op.pos - d.pos) <= 2

    def finalize(self):
        for op in self.ops:
            for d in op.deps:
                if not d.is_dma and self._needs(op, d):
                    d.signal = True
        for e in self.ENG:
            c = 0
            for op in self.eng_ops[e]:
                if op.signal:
                    c += 1
                op.count = c

    def emit_engine(self, e, eng, eng_sem, dma_sem, final_waits=()):
        waited = {}
        for op in self.eng_ops[e]:
            waits = {}
            for d in op.deps:
                if not self._needs(op, d):
                    continue
                if d.is_dma:
                    key = ("dma", d.dma_key)
                    val = d.dma_val
                else:
                    key = ("eng", d.eng)
                    val = d.count
                if val > waits.get(key, 0):
                    waits[key] = val
            for key, val in waits.items():
                if waited.get(key, 0) >= val:
                    continue
                waited[key] = val
                sem = dma_sem[key[1]] if key[0] == "dma" else eng_sem[key[1]]
                eng.wait_ge(sem, val)
            ins = op.fn(eng)
            if op.is_dma:
                ins.then_inc(dma_sem[op.dma_key], 16)
            elif op.signal:
                ins.then_inc(eng_sem[e], 1)
        for key in final_waits:
            eng.wait_ge(dma_sem[key], 16 * self.dma_counts[key])


class TileCfg:
    def __init__(self, name, T, nseg, L, segseq, blocks, xsrc, outblocks, kind, tidx):
        self.name = name
        self.T = T
        self.nseg = nseg
        self.L = L
        self.segseq = segseq
        self.blocks = blocks
        self.xsrc = xsrc
        self.outblocks = outblocks
        self.kind = kind
        self.tidx = tidx
        self.par = 0


def build_program(n_ptiles=8, do_s=True, stop_after=None, interleave=True):
    nc = bass.Bass("TRN2", target_bir_lowering=False)

    def din(name, shape):
        return nc.dram_tensor(name, shape, F32, kind="ExternalInput").ap()

    def dout(name, shape):
        return nc.dram_tensor(name, shape, F32, kind="ExternalOutput").ap()

    xs_d = din("xs", [192, D])
    xp_d = din("xp", [PCH, D])
    cT_d = din("cT", [128, NFT * NSEQ])
    cc_d = din("ccache", [128, 4 * 2 * CH])
    pc_d = din("pcache", [128, 4 * 2 * PH])
    vecs_d = din("vecs", [128, NV])
    ada_d = din("ada_w", [36, 128, 2048])
    w1_d = din("w1", [2, 11, 128, 4096])
    wd_d = din("wd", [2, 8, 128, DFF])
    wmi_d = din("wmi", [3, 128, 4096])
    wpo_d = din("wpo", [128, 512])
    wo_d = din("wo", [2, 128, 4096])

    yp_d = dout("yp", [PCH, D])
    ys_d = dout("ys", [192, D])
    scp_d = dout("scp", [CH, 512])
    spp_d = dout("spp", [PH, 512])
    scs_d = dout("scs", [2, CH, 512])
    sps_d = dout("sps", [2, PH, 512])

    S = Sched()
    dry = [False]

    class _Dummy:
        def add(self, *a, **k):
            return None
    _dummy = _Dummy()

    def ADD(*a, **k):
        return (_dummy if dry[0] else S).add(*a, **k)
    es = contextlib.ExitStack()

    def sb(name, shape, dt=F32):
        return es.enter_context(nc.sbuf_tensor("s_" + name, shape, dt))

    with es:
        vecs = sb("vecs", [128, NV])
        ones = sb("ones", [128, 128], BF16)
        cT = sb("cT", [128, NFT, NSEQ])
        sc = sb("sc", [128, NFT, NSEQ], BF16)
        modsb = sb("modsb", [128, 72, NSEQ])
        par = sb("par", [128, 264])
        xt = sb("xt", [128, 4, D])
        yt = sb("yt", [128, 2, D])
        xfm2 = [sb("xfm0", [128, NFT, 512]), sb("xfm1", [128, NFT, 512])]
        u = sb("u", [128, NFT, 512], BF16)
        h = sb("h", [128, NKF, 512], BF16)
        sg = sb("sg", [128, 2, 512])
        sq = sb("sq", [128, 2, 512], BF16)
        rr = sb("rr", [128, 2, 512], BF16)
        mean = sb("mean", [128, 512])
        rstd = sb("rstd", [128, 512])
        cin = sb("cin", [128, 4, 544])
        pin = sb("pin", [128, 4, 528])
        acc = sb("acc", [128, 4, 512])
        sig = sb("sig", [128, 2, 512])
        act = sb("act", [128, 8, 512], BF16)
        pooled = sb("pooled", [128, 4, 512], BF16)
        sA = sb("sA", [128, 528])
        sB = sb("sB", [128, 528])
        t16 = sb("t16", [128, 16])
        u2 = sb("u2", [128, NFT, 512], BF16)
        ringF = sb("ringF", [128, NSF, SLOT_ELEMS], BF16)
        ringM = sb("ringM", [128, NSM, SLOT_ELEMS], BF16)
        ps = [es.enter_context(nc.psum_tensor(f"ps{i}", [128, 512], F32)) for i in range(8)]

        ident = vecs[:, _off["ident"]:_off["ident"] + 128]

        def vcol(name, idx):
            o = _off[name] + idx
            return vecs[:, o:o + 1]

        pcol = [0]

        def palloc(n):
            o = pcol[0]
            pcol[0] += n
            return o

        P_s1 = [[palloc(8) for s in range(3)] for k in range(3)]
        P_gp = [[palloc(8) for s in range(3)] for k in range(3)]
        P_tB = [[palloc(8) for s in range(3)] for k in range(3)]
        P_xb0 = [palloc(8) for s in range(3)]
        P_tmp = palloc(8)
        P_hb = palloc(12)

        def pc(o, ft):
            return par[:, o + ft:o + ft + 1]

        def modv(j, s):
            return modsb[:, j * 8:(j + 1) * 8, s]

        def modc(j, ft, s):
            return modsb[:, j * 8 + ft, s:s + 1]

        ADD("sp", lambda e: e.dma_start(out=vecs[:], in_=vecs_d), writes=[("vecs",)], dma="d_vecs")
        ADD("sp", lambda e: e.dma_start(out=cT[:].rearrange("p k s -> p (k s)"), in_=cT_d),
              writes=[("cT",)], dma="d_cT")
        ADD("act", lambda e: e.activation(out=ones[:], in_=ident, func=AF.Identity, scale=0.0, bias=1.0),
              reads=[("vecs",)], writes=[("ones",)])
        ADD("dve", lambda e: e.tensor_scalar(out=par[:, P_hb:P_hb + 12],
                                             in0=vecs[:, _off["mixb"]:_off["mixb"] + 12],
                                             scalar1=0.5, scalar2=None, op0=ALU.mult),
            reads=[("vecs",)], writes=[("par", "hb")])
        ADD("act", lambda e: e.activation(out=sc[:], in_=cT[:], func=AF.Silu),
              reads=[("cT",)], writes=[("sc",)])


        ada_issued = [0]

        ADA_LATE = 24

        def ada_stage(ci):
            if ci >= ADA_LATE:
                slot = 4 + (ci % 2)
                stage = yt[:, ci % 2, :].bitcast(BF16).rearrange("p (k n) -> p k n", k=8)
                return slot, stage, [("yt", ci % 2, 0), ("yt", ci % 2, 1)]
            slot = ci % 4
            sbuf_, skn = (act, "act") if slot < 2 else (u2, "u2")
            h0 = (slot % 2) * 4
            stage = sbuf_[:, h0:h0 + 4, :].rearrange("p a (k n) -> p (a k) n", k=2)
            skeys = [(skn, h0 + i_) for i_ in range(4)]
            return slot, stage, skeys

        def ada_dma_upto(ci_):
            if dry[0]:
                return
            while ada_issued[0] < 36 and ada_issued[0] <= ci_ + (3 if ada_issued[0] < ADA_LATE else 1):
                cj = ada_issued[0]
                slot_, stage_, skeys_ = ada_stage(cj)

                def _ld(e, stage_=stage_, cj=cj):
                    return e.dma_start(out=stage_.rearrange("p k n -> p (k n)"), in_=ada_d[cj])
                ADD("pool", _ld, writes=skeys_, dma=("ada", slot_))
                ada_issued[0] += 1

        def g_ada(c0, c1):
          for ci in range(c0, c1):
              ada_dma_upto(ci)
              slot, stage, skeys = ada_stage(ci)
              for n in range(2):
                  tile_i = ci * 2 + n
                  j = tile_i // 8
                  bank = 6 + (j % 2)

                  def _mm(e, stage=stage, n=n, tile_i=tile_i, bank=bank):
                      ins = None
                      for kt in range(NFT):
                          ins = e.matmul(out=ps[bank][:, tile_i * NSEQ:(tile_i + 1) * NSEQ],
                                         lhsT=stage[:, kt, n * 128:(n + 1) * 128],
                                         rhs=sc[:, kt, :], start=(kt == 0), stop=(kt == NFT - 1))
                      return ins
                  ADD("pe", _mm, reads=skeys + [("sc",)],
                        writes=[("ps", bank)])

                  def _ev(e, tile_i=tile_i, bank=bank):
                      return e.tensor_scalar(out=modsb[:, tile_i, :],
                                             in0=ps[bank][:, tile_i * NSEQ:(tile_i + 1) * NSEQ],
                                             scalar1=vecs[:, _off["ada_b"] + tile_i:_off["ada_b"] + tile_i + 1],
                                             scalar2=None, op0=ALU.add)
                  ADD("dve", _ev, reads=[("ps", bank), ("vecs",)], writes=[("mod", j)])
                  if tile_i % 8 == 7:
                      k = j // 3
                      if j % 3 == 2:
                          coef = (1.0 if k == 1 else 0.5) / ALPHA
                          for s in range(3):
                              ADD("dve", lambda e, k=k, s=s: e.tensor_scalar(
                                  out=par[:, P_s1[k][s]:P_s1[k][s] + 8], in0=modv(3 * k + 1, s),
                                  scalar1=1.0, scalar2=None, op0=ALU.add),
                                  reads=[("mod", 3 * k + 1)], writes=[("par", k)])
                              ADD("dve", lambda e, k=k, s=s, coef=coef: e.tensor_scalar(
                                  out=par[:, P_gp[k][s]:P_gp[k][s] + 8], in0=modv(3 * k + 2, s),
                                  scalar1=coef, scalar2=None, op0=ALU.mult),
                                  reads=[("mod", 3 * k + 2)], writes=[("par", k)])
                              if k >= 1:
                                  lb = _off["ln_b"] + (k - 1) * 8
                                  ADD("dve", lambda e, k=k, s=s, lb=lb: e.tensor_tensor(
                                      out=par[:, P_tmp:P_tmp + 8], in0=vecs[:, lb:lb + 8],
                                      in1=par[:, P_s1[k][s]:P_s1[k][s] + 8], op=ALU.mult),
                                      reads=[("par", k), ("vecs",)], writes=[("ptmp",)])
                                  ADD("dve", lambda e, k=k, s=s: e.tensor_tensor(
                                      out=par[:, P_tB[k][s]:P_tB[k][s] + 8], in0=par[:, P_tmp:P_tmp + 8],
                                      in1=modv(3 * k, s), op=ALU.add),
                                      reads=[("ptmp",), ("mod", 3 * k)], writes=[("par", k)])
                              if k == 1:
                                  bo = _off["b_out"]
                                  lb0 = _off["ln_b"]
                                  ADD("dve", lambda e, s=s, bo=bo: e.tensor_tensor(
                                      out=par[:, P_tmp:P_tmp + 8], in0=par[:, P_gp[1][s]:P_gp[1][s] + 8],
                                      in1=vecs[:, bo:bo + 8], op=ALU.mult),
                                      reads=[("par", 1), ("vecs",)], writes=[("ptmp",)])
                                  ADD("dve", lambda e, s=s, lb0=lb0: e.tensor_tensor(
                                      out=par[:, P_xb0[s]:P_xb0[s] + 8], in0=par[:, P_tmp:P_tmp + 8],
                                      in1=vecs[:, lb0:lb0 + 8], op=ALU.add),
                                      reads=[("ptmp",), ("vecs",)], writes=[("par", 1)])
              yield 7.0

        def f_chunks(f):
            return [(w1_d[f, c], 4096) for c in range(11)] + [(wd_d[f, mo], DFF) for mo in range(8)]
        m_chunks = [(wmi_d[c], 4096) for c in range(3)] + [(wpo_d, 512)] + [(wo_d[c], 4096) for c in range(2)]
        NFC = 19
        NMC = 6
        f_order = []
        f_index = {}
        wst = {"F": 0, "M": 0, "ntiles": 0}

        def _prefetch(which, upto):
            if which == "F":
                total = len(f_order) * NFC
            else:
                total = wst["ntiles"] * NMC
            upto = min(upto, total - 1)
            while wst[which] <= upto:
                gi = wst[which]
                if which == "F":
                    tk = f_order[gi // NFC]
                    src, n = f_chunks(0 if tk[1] == 0 else 1)[gi % NFC]
                    slot = gi % NSF
                    ringt = ringF
                else:
                    src, n = m_chunks[gi % NMC]
                    slot = gi % NSM
                    ringt = ringM

                def _ld(e, src=src, n=n, slot=slot, ringt=ringt):
                    return e.dma_start(out=ringt[:, slot, 0:n], in_=src)
                ADD("pool", _ld, writes=[("ring" + which, slot)], dma=("ring" + which, slot))
                wst[which] += 1

        def wslotF(tidx, k, local):
            if dry[0]:
                return 0
            gi = f_index[(tidx, k)] * NFC + local
            _prefetch("F", gi + NSF - 1)
            return gi % NSF

        def wslotM(tidx, local):
            if dry[0]:
                return 0
            gi = tidx * NMC + local
            _prefetch("M", gi + NSM - 1)
            return gi % NSM

        CH_WMI = 0
        CH_WPO = 3
        CH_WO = 4

        def segs(cfg):
            return [(si, cfg.segseq[si], si * cfg.L, (si + 1) * cfg.L) for si in range(cfg.nseg)]

        x_issued = set()

        def issue_x(cfg):
            if dry[0] or cfg.tidx in x_issued:
                return
            x_issued.add(cfg.tidx)
            for bi, (c0, rows) in enumerate(cfg.blocks):
                def _ld(e, bi=bi, c0=c0, rows=rows, cfg=cfg):
                    return e.dma_start(out=xt[:rows, bi, :], in_=cfg.xsrc[c0:c0 + rows, :])
                ADD("sp", _ld, writes=[("xt", bi)], dma=("xin", bi))

        def load_in(cfg):
            T = cfg.T
            xfm = xfm2[cfg.par]
            XK = lambda ft_: ("xfm", cfg.par, ft_)
            issue_x(cfg)
            for ft in range(NFT):
                bank = 4 + (ft % 2)

                def _tr(e, ft=ft, bank=bank):
                    ins = None
                    for bi, (c0, rows) in enumerate(cfg.blocks):
                        ins = e.transpose(out=ps[bank][:, c0:c0 + rows],
                                          in_=xt[:rows, bi, ft * 128:(ft + 1) * 128],
                                          identity=ident[:rows, :rows])
                    return ins
                ADD("pe", _tr, reads=[("xt", bi) for bi in range(len(cfg.blocks))] + [("vecs",)],
                      writes=[("ps", bank)])
                ADD("act", lambda e, ft=ft, bank=bank: e.activation(out=xfm[:, ft, 0:T], in_=ps[bank][:, 0:T],
                                                                   func=AF.Copy),
                      reads=[("ps", bank)], writes=[XK(ft)])
                for (si, s, a, b) in segs(cfg):
                    ADD("act", lambda e, ft=ft, bank=bank, s=s, a=a, b=b: e.activation(
                        out=u[:, ft, a:b], in_=ps[bank][:, a:b], func=AF.Identity,
                        scale=pc(P_s1[0][s], ft), bias=modc(0, ft, s)),
                        reads=[("ps", bank), ("par", 0), ("mod", 0)], writes=[("u", ft)])
                yield 1.0 * T / 512
            if not dry[0] and cfg.tidx + 1 < len(tiles):
                issue_x(tiles[cfg.tidx + 1])

        def layer_norm(cfg, k, hook=None):
            T = cfg.T
            xfm = xfm2[cfg.par]
            XK = lambda ft_: ("xfm", cfg.par, ft_)
            ubuf, ukn = (act, "act") if k == 0 else (u2, "u2")
            if k == 1:
                yield ("need", ("f2in", cfg.tidx - 1))
            yield ("lock", "ln")
            for ft in range(NFT):
                r = ft % 2
                ADD("dve", lambda e, ft=ft, r=r: e.tensor_copy(out=rr[:, r, 0:T], in_=xfm[:, ft, 0:T]),
                      reads=[XK(ft)], writes=[("rr", r)])
                ADD("act", lambda e, ft=ft, r=r: e.activation(out=sq[:, r, 0:T], in_=xfm[:, ft, 0:T], func=AF.Square),
                      reads=[XK(ft)], writes=[("sq", r)])

                def _st(e, ft=ft, r=r):
                    e.matmul(out=ps[6][:, 0:T], lhsT=ones[:], rhs=rr[:, r, 0:T],
                             start=(ft == 0), stop=(ft == NFT - 1))
                    return e.matmul(out=ps[7][:, 0:T], lhsT=ones[:], rhs=sq[:, r, 0:T],
                                    start=(ft == 0), stop=(ft == NFT - 1))
                ADD("pe", _st, reads=[("rr", r), ("sq", r), ("ones",)], writes=[("ps", 6), ("ps", 7)])
                yield 0.6 * T / 512
            stats(cfg, 1.0 / D, 0)
            yield 6.0
            def opA(ft):
                ADD("dve", lambda e, ft=ft: e.tensor_tensor(out=xfm[:, ft, 0:T], in0=xfm[:, ft, 0:T],
                                                               in1=mean[:, 0:T], op=ALU.subtract),
                      reads=[XK(ft), ("mean",)], writes=[XK(ft)])

            def opB(ft):
                ADD("dve", lambda e, ft=ft: e.scalar_tensor_tensor(
                    out=xfm[:, ft, 0:T], in0=xfm[:, ft, 0:T], scalar=vcol("ln_g", k * 8 + ft),
                    in1=rstd[:, 0:T], op0=ALU.mult, op1=ALU.mult),
                    reads=[XK(ft), ("rstd",), ("vecs",)], writes=[XK(ft)])
                if k < 2:
                    for (si, s, a, b) in segs(cfg):
                        ADD("act", lambda e, ft=ft, s=s, a=a, b=b: e.activation(
                            out=ubuf[:, ft, a:b], in_=xfm[:, ft, a:b], func=AF.Identity,
                            scale=pc(P_s1[k + 1][s], ft), bias=pc(P_tB[k + 1][s], ft)),
                            reads=[XK(ft), ("par", k + 1)], writes=[(ukn, ft)])

            def opX(ft):
                if k == 0:
                    for (si, s, a, b) in segs(cfg):
                        ADD("dve", lambda e, ft=ft, s=s, a=a, b=b: e.tensor_scalar(
                            out=xfm[:, ft, a:b], in0=xfm[:, ft, a:b], scalar1=pc(P_xb0[s], ft),
                            scalar2=None, op0=ALU.add),
                            reads=[XK(ft), ("par", 1)], writes=[XK(ft)])
                else:
                    ADD("dve", lambda e, ft=ft: e.tensor_scalar(
                        out=xfm[:, ft, 0:T], in0=xfm[:, ft, 0:T], scalar1=vcol("ln_b", k * 8 + ft),
                        scalar2=None, op0=ALU.add),
                        reads=[XK(ft), ("vecs",)], writes=[XK(ft)])

            for ft in range(3):
                opA(ft)
            for ft in range(NFT):
                opB(ft)
                if ft + 3 < NFT:
                    opA(ft + 3)
                if ft >= 1:
                    opX(ft - 1)
                if ft == 4 and hook is not None:
                    yield from hook
                yield 1.7 * T / 512
            opX(NFT - 1)
            yield ("unlock", "ln")

        def stats(cfg, inv_n, eps_idx):
            T = cfg.T
            ADD("act", lambda e: e.activation(out=mean[:, 0:T], in_=ps[6][:, 0:T], func=AF.Identity, scale=inv_n),
                  reads=[("ps", 6)], writes=[("mean",)])
            ADD("dve", lambda e: e.tensor_tensor(out=sig[:, 1, 0:T], in0=mean[:, 0:T], in1=mean[:, 0:T], op=ALU.mult),
                  reads=[("mean",)], writes=[("sig", 1)])
            ADD("dve", lambda e: e.scalar_tensor_tensor(out=rstd[:, 0:T], in0=ps[7][:, 0:T], scalar=inv_n,
                                                          in1=sig[:, 1, 0:T], op0=ALU.mult, op1=ALU.subtract),
                  reads=[("ps", 7), ("sig", 1)], writes=[("rstd",)])
            ADD("act", lambda e: e.activation(out=rstd[:, 0:T], in_=rstd[:, 0:T], func=AF.Sqrt,
                                                bias=vcol("eps", eps_idx)),
                  reads=[("rstd",), ("vecs",)], writes=[("rstd",)])
            ADD("dve", lambda e: e.reciprocal(out=rstd[:, 0:T], in_=rstd[:, 0:T]),
                  reads=[("rstd",)], writes=[("rstd",)])

        def ffn(cfg, k):
            T = cfg.T
            f = 0 if k == 0 else 1
            xfm = xfm2[cfg.par]
            XK = lambda ft_: ("xfm", cfg.par, ft_)
            uin, ukn_ = (u, "u") if k == 0 else (u2, "u2")
            ukeys = [(ukn_, ft) for ft in range(NFT)]
            for j in range(NKF):
                c = j // 2
                slot = wslotF(cfg.tidx, k, c)
                wv = ringF[:, slot, :].rearrange("p (k m c) -> p k m c", k=8, m=4)
                bg = (j % 2) * 2
                bv = bg + 1
                for which, bank in ((0, bg), (1, bv)):
                    mt = (j % 2) * 2 + which

                    def _mm(e, wv=wv, mt=mt, bank=bank):
                        ins = None
                        for kt in range(NFT):
                            ins = e.matmul(out=ps[bank][:, 0:T], lhsT=wv[:, kt, mt, :], rhs=uin[:, kt, 0:T],
                                           start=(kt == 0), stop=(kt == NFT - 1))
                        return ins
                    ADD("pe", _mm, reads=ukeys + [("ringF", slot)], writes=[("ps", bank)])
                r = j % 2
                ADD("act", lambda e, r=r, bg=bg: e.activation(out=sg[:, r, 0:T], in_=ps[bg][:, 0:T], func=AF.Silu),
                      reads=[("ps", bg)], writes=[("sg", r)])
                ADD("dve", lambda e, r=r, bv=bv, j=j: e.tensor_tensor(out=h[:, j, 0:T], in0=sg[:, r, 0:T],
                                                                         in1=ps[bv][:, 0:T], op=ALU.mult),
                      reads=[("sg", r), ("ps", bv)], writes=[("h", j)])
                yield 3.9 * (0.5 + 0.5 * T / 512)
            if k == 2:
                yield ("mark", ("f2in", cfg.tidx))
            for mo in range(NFT):
                slot = wslotF(cfg.tidx, k, 11 + mo)
                wv = ringF[:, slot, 0:DFF].rearrange("p (k c) -> p k c", k=NKF)
                bank = 4 + (mo % 2)
                for half in range(2):
                    k0, k1 = (0, 11) if half == 0 else (11, 22)

                    def _mm(e, wv=wv, bank=bank, k0=k0, k1=k1):
                        ins = None
                        for kf in range(k0, k1):
                            ins = e.matmul(out=ps[bank][:, 0:T], lhsT=wv[:, kf, :], rhs=h[:, kf, 0:T],
                                           start=(kf == 0), stop=(kf == NKF - 1))
                        return ins
                    ADD("pe", _mm, reads=[("h", kf) for kf in range(k0, k1)] + [("ringF", slot)],
                          writes=[("ps", bank)])
                for (si, s, a, b) in segs(cfg):
                    ADD("dve", lambda e, mo=mo, bank=bank, s=s, a=a, b=b: e.scalar_tensor_tensor(
                        out=xfm[:, mo, a:b], in0=ps[bank][:, a:b], scalar=pc(P_gp[k][s], mo),
                        in1=xfm[:, mo, a:b], op0=ALU.mult, op1=ALU.add),
                        reads=[("ps", bank), XK(mo), ("par", k)], writes=[XK(mo)])
                yield 5.4 * (0.5 + 0.5 * T / 512)

        def seg_view(ap2d, nseg, w, lo, hi):
            return ap2d[:, 0:nseg * w].rearrange("p (s w) -> p s w", s=nseg)[:, :, lo:hi]

        def mixer(cfg):
            T, nseg, L = cfg.T, cfg.nseg, cfg.L
            xfm = xfm2[cfg.par]
            XK = lambda ft_: ("xfm", cfg.par, ft_)
            CWD = CH + L
            PWD = PH + L
            allcin = [("cin", i) for i in range(4)]
            allpin = [("pin", i) for i in range(4)]
            if cfg.kind == "S":
                for i in range(4):
                    ADD("sp", lambda e, i=i: e.dma_start(
                        out=cin[:, i, 0:2 * CWD].rearrange("p (s w) -> p s w", s=2)[:, :, 0:CH],
                        in_=cc_d[:, i * 2 * CH:(i + 1) * 2 * CH].rearrange("p (s t) -> p s t", s=2)),
                        writes=[("cin", i)], dma=("d_cc", i))
                    ADD("sp", lambda e, i=i: e.dma_start(
                        out=pin[:, i, 0:2 * PWD].rearrange("p (s w) -> p s w", s=2)[:, :, 0:PH],
                        in_=pc_d[:, i * 2 * PH:(i + 1) * 2 * PH].rearrange("p (s t) -> p s t", s=2)),
                        writes=[("pin", i)], dma=("d_pc", i))
                ADD("pool", lambda e: e.tensor_copy(out=cin[:, :, 2 * CWD:2 * CWD + CH], in_=cin[:, :, 512:512 + CH]),
                      reads=allcin, writes=allcin)
                ADD("pool", lambda e: e.tensor_copy(out=pin[:, :, 2 * PWD:2 * PWD + PH], in_=pin[:, :, 512:512 + PH]),
                      reads=allpin, writes=allpin)
            elif cfg.kind == "P0":
                ADD("pool", lambda e: e.memset(cin[:, :, 0:CH], 0.0), writes=allcin)
                ADD("pool", lambda e: e.memset(pin[:, :, 0:PH], 0.0), writes=allpin)
            else:
                ADD("pool", lambda e: e.tensor_copy(out=cin[:, :, 0:CH], in_=cin[:, :, 512:512 + CH]),
                      reads=allcin, writes=allcin)
                ADD("pool", lambda e: e.tensor_copy(out=pin[:, :, 0:PH], in_=pin[:, :, 512:512 + PH]),
                      reads=allpin, writes=allpin)
            ukeys = [("act", ft) for ft in range(NFT)]

            def inproj(local_chunk, mt, bank):
                slot = wslotM(cfg.tidx, CH_WMI + local_chunk)
                wv = ringM[:, slot, :].rearrange("p (k m c) -> p k m c", k=8, m=4)

                def _mm(e, wv=wv, mt=mt, bank=bank):
                    ins = None
                    for kt in range(NFT):
                        ins = e.matmul(out=ps[bank][:, 0:T], lhsT=wv[:, kt, mt, :], rhs=act[:, kt, 0:T],
                                       start=(kt == 0), stop=(kt == NFT - 1))
                    return ins
                ADD("pe", _mm, reads=ukeys + [("ringM", slot)], writes=[("ps", bank)])

            for i in range(4):
                c = i // 2
                bb = (i % 2) * 2
                ba = bb + 1
                inproj(c, (i % 2) * 2, bb)
                inproj(c, (i % 2) * 2 + 1, ba)
                r = 0
                mb = _off["mixb"] + c * 4 + (i % 2) * 2
                hbc = P_hb + c * 4 + (i % 2) * 2
                ADD("act", lambda e, r=r, bb=bb, hbc=hbc: e.activation(
                    out=sig[:, r, 0:T], in_=ps[bb][:, 0:T], func=AF.Tanh, scale=0.5, bias=par[:, hbc:hbc + 1]),
                    reads=[("ps", bb), ("par", "hb")], writes=[("sig", r)])
                ADD("dve", lambda e, r=r: e.tensor_scalar(
                    out=sig[:, r, 0:T], in0=sig[:, r, 0:T], scalar1=1.0, scalar2=0.5, op0=ALU.add, op1=ALU.mult),
                    reads=[("sig", r)], writes=[("sig", r)])
                ADD("dve", lambda e, i=i, r=r, ba=ba, mb=mb: e.scalar_tensor_tensor(
                    out=seg_view(cin[:, i, :], nseg, CWD, CH, CWD),
                    in0=ps[ba][:, 0:T].rearrange("p (s l) -> p s l", s=nseg),
                    scalar=vecs[:, mb + 1:mb + 2],
                    in1=sig[:, r, 0:T].rearrange("p (s l) -> p s l", s=nseg),
                    op0=ALU.add, op1=ALU.mult),
                    reads=[("ps", ba), ("sig", r), ("vecs",)], writes=[("cin", i)])
                yield 4.0 * (0.5 + 0.5 * T / 512)
            for i in range(4):
                bank = i % 4
                inproj(2, i, bank)
                mb = _off["mixb"] + 8 + i
                ADD("act", lambda e, i=i, bank=bank, mb=mb: e.activation(
                    out=seg_view(pin[:, i, :], nseg, PWD, PH, PWD),
                    in_=ps[bank][:, 0:T].rearrange("p (s l) -> p s l", s=nseg),
                    func=AF.Identity, bias=vecs[:, mb:mb + 1]),
                    reads=[("ps", bank), ("vecs",)], writes=[("pin", i)])
                yield 2.0 * (0.5 + 0.5 * T / 512)
            if cfg.kind == "P0":
                ADD("dve", lambda e: e.tensor_scalar(out=cin[:, :, CH:CH + 64], in0=cin[:, :, CH:CH + 64],
                                                       scalar1=vcol("mask", 0), scalar2=None, op0=ALU.mult),
                      reads=allcin + [("vecs",)], writes=allcin)
                ADD("dve", lambda e: e.tensor_scalar(out=pin[:, :, PH:PH + 64], in0=pin[:, :, PH:PH + 64],
                                                       scalar1=vcol("mask", 0), scalar2=None, op0=ALU.mult),
                      reads=allpin + [("vecs",)], writes=allpin)
            if cfg.kind == "S":
                emit_state(cfg, 0, scs_d[0], sps_d[0])
                emit_state(cfg, 1, scs_d[1], sps_d[1])
                emit_state(cfg, 2, scp_d, spp_d)
            W = PWD
            for g in range(4):
                pv = lambda lo, hi, g=g: seg_view(pin[:, g, :], nseg, W, lo, hi)
                av_ = lambda lo, hi: seg_view(sA[:], nseg, W, lo, hi)
                bv_ = lambda lo, hi: seg_view(sB[:], nseg, W, lo, hi)
                ADD("pool", lambda e, pv=pv, av_=av_: e.tensor_tensor(
                    out=av_(1, W), in0=pv(1, W), in1=pv(0, W - 1), op=ALU.add),
                    reads=[("pin", g)], writes=[("sA",)])
                last = av_
                lastkey = ("sA",)
                if g >= 1:
                    ADD("pool", lambda e, av_=av_, bv_=bv_: e.tensor_tensor(
                        out=bv_(3, W), in0=av_(3, W), in1=av_(1, W - 2), op=ALU.add),
                        reads=[("sA",)], writes=[("sB",)])
                    last, lastkey = bv_, ("sB",)
                if g >= 2:
                    ADD("pool", lambda e, av_=av_, bv_=bv_: e.tensor_tensor(
                        out=av_(7, W), in0=bv_(7, W), in1=bv_(3, W - 4), op=ALU.add),
                        reads=[("sB",)], writes=[("sA",)])
                    last, lastkey = av_, ("sA",)
                if g >= 3:
                    ADD("pool", lambda e, av_=av_, bv_=bv_: e.tensor_tensor(
                        out=bv_(15, W), in0=av_(15, W), in1=av_(7, W - 8), op=ALU.add),
                        reads=[("sA",)], writes=[("sB",)])
                    last, lastkey = bv_, ("sB",)
                wnd = float(2 ** (g + 1))
                ADD("dve", lambda e, g=g, last=last, pv=pv, wnd=wnd: e.scalar_tensor_tensor(
                    out=pooled[:, g, 0:T].rearrange("p (s l) -> p s l", s=nseg),
                    in0=last(PH, W), scalar=1.0 / wnd, in1=pv(PH, W), op0=ALU.mult, op1=ALU.subtract),
                    reads=[lastkey, ("pin", g)], writes=[("pooled", g)])
                if cfg.kind == "P0":
                    ic = _off["invcnt"] + g * 16
                    lastflat = sA if lastkey == ("sA",) else sB
                    ADD("pool", lambda e, lastflat=lastflat, ic=ic: e.tensor_tensor(
                        out=t16[:], in0=lastflat[:, PH + 64:PH + 80], in1=vecs[:, ic:ic + 16], op=ALU.mult),
                        reads=[lastkey, ("vecs",)], writes=[("t16",)])
                    ADD("pool", lambda e, g=g: e.tensor_tensor(
                        out=pooled[:, g, 64:80], in0=t16[:], in1=pin[:, g, PH + 64:PH + 80], op=ALU.subtract),
                        reads=[("t16",), ("pin", g), ("pooled", g)], writes=[("pooled", g)])
                slot = wslotM(cfg.tidx, CH_WPO)
                wv = ringM[:, slot, 0:512].rearrange("p (g d) -> p g d", g=4)
                bank = 4 + (g % 2)
                ADD("pe", lambda e, g=g, wv=wv, bank=bank: e.matmul(
                    out=ps[bank][:, 0:T], lhsT=wv[:, g, :], rhs=pooled[:, g, 0:T], start=True, stop=True),
                    reads=[("pooled", g), ("ringM", slot)], writes=[("ps", bank)])
                ADD("dve", lambda e, g=g, bank=bank: e.tensor_scalar(
                    out=act[:, 4 + g, 0:T], in0=ps[bank][:, 0:T], scalar1=vcol("pool_b", g),
                    scalar2=vcol("pool_s", g), op0=ALU.add, op1=ALU.mult),
                    reads=[("ps", bank), ("vecs",)], writes=[("act", 4 + g)])
                yield 1.5 * T / 512
            cvs = [(lambda k, i=i: seg_view(cin[:, i, :], nseg, CWD, k, k + L)) for i in range(4)]
            avs = [acc[:, i, 0:T].rearrange("p (s l) -> p s l", s=nseg) for i in range(4)]
            for i in range(4):
                ADD("dve", lambda e, i=i: e.tensor_scalar(
                    out=avs[i], in0=cvs[i](0), scalar1=vcol("conv_w", 0 * 4 + i), scalar2=vcol("conv_b", i),
                    op0=ALU.mult, op1=ALU.add),
                    reads=[("cin", i), ("vecs",)], writes=[("acc", i)])
                yield 0.4 * T / 512
            for k in range(1, CW):
                for i in range(4):
                    ADD("dve", lambda e, i=i, k=k: e.scalar_tensor_tensor(
                        out=avs[i], in0=cvs[i](k), scalar=vcol("conv_w", k * 4 + i), in1=avs[i],
                        op0=ALU.mult, op1=ALU.add),
                        reads=[("cin", i), ("acc", i), ("vecs",)], writes=[("acc", i)])
                    yield 0.62 * T / 512
            yield ("lock", "ln")
            for i in range(4):
                ADD("dve", lambda e, i=i: e.tensor_copy(out=rr[:, i % 2, 0:T], in_=acc[:, i, 0:T]),
                      reads=[("acc", i)], writes=[("rr", i % 2)])
                ADD("act", lambda e, i=i: e.activation(out=sq[:, i % 2, 0:T], in_=acc[:, i, 0:T], func=AF.Square),
                      reads=[("acc", i)], writes=[("sq", i % 2)])

                def _st(e, i=i):
                    e.matmul(out=ps[6][:, 0:T], lhsT=ones[:], rhs=rr[:, i % 2, 0:T],
                             start=(i == 0), stop=(i == 3))
                    return e.matmul(out=ps[7][:, 0:T], lhsT=ones[:], rhs=sq[:, i % 2, 0:T],
                                    start=(i == 0), stop=(i == 3))
                ADD("pe", _st, reads=[("rr", i % 2), ("sq", i % 2), ("ones",)], writes=[("ps", 6), ("ps", 7)])
                yield 0.6 * T / 512
            stats(cfg, 1.0 / 512, 1)
            yield 6.0
            for i in range(4):
                ADD("dve", lambda e, i=i: e.tensor_tensor(out=acc[:, i, 0:T], in0=acc[:, i, 0:T],
                                                             in1=mean[:, 0:T], op=ALU.subtract),
                      reads=[("acc", i), ("mean",)], writes=[("acc", i)])
                ADD("dve", lambda e, i=i: e.scalar_tensor_tensor(
                    out=acc[:, i, 0:T], in0=acc[:, i, 0:T], scalar=vcol("cln_g", i), in1=rstd[:, 0:T],
                    op0=ALU.mult, op1=ALU.mult),
                    reads=[("acc", i), ("rstd",), ("vecs",)], writes=[("acc", i)])
                ADD("act", lambda e, i=i: e.activation(out=act[:, i, 0:T], in_=acc[:, i, 0:T], func=AF.Silu,
                                                         bias=vcol("cln_b", i)),
                      reads=[("acc", i), ("vecs",)], writes=[("act", i)])
                yield 1.9 * T / 512
            yield ("unlock", "ln")
            akeys = [("act", kk) for kk in range(8)]
            for mo in range(NFT):
                slot = wslotM(cfg.tidx, CH_WO + mo // 4)
                wv = ringM[:, slot, :].rearrange("p (m k c) -> p m k c", m=4, k=8)
                bank = 4 + (mo % 2)

                def _mm(e, wv=wv, mo=mo, bank=bank):
                    ins = None
                    for kk in range(8):
                        ins = e.matmul(out=ps[bank][:, 0:T], lhsT=wv[:, mo % 4, kk, :], rhs=act[:, kk, 0:T],
                                       start=(kk == 0), stop=(kk == 7))
                    return ins
                ADD("pe", _mm, reads=akeys + [("ringM", slot)], writes=[("ps", bank)])
                for (si, s, a, b) in segs(cfg):
                    ADD("dve", lambda e, mo=mo, bank=bank, s=s, a=a, b=b: e.scalar_tensor_tensor(
                        out=xfm[:, mo, a:b], in0=ps[bank][:, a:b], scalar=pc(P_gp[1][s], mo),
                        in1=xfm[:, mo, a:b], op0=ALU.mult, op1=ALU.add),
                        reads=[("ps", bank), XK(mo), ("par", 1)], writes=[XK(mo)])
                yield 2.0 * (0.5 + 0.5 * T / 512)

        st_count = [0]

        def emit_state(cfg, seg, conv_dst, pool_dst):
            L = cfg.L
            CWD = CH + L
            PWD = PH + L
            for (buf, key, wd, hist, dst, dkey) in ((cin, "cin", CWD, CH, conv_dst, "stc"),
                                                     (pin, "pin", PWD, PH, pool_dst, "stp")):
                r = st_count[0] % 2
                st_count[0] += 1
                bank = 4 + r
                c0 = seg * wd + L

                def _tr(e, buf=buf, bank=bank, c0=c0, hist=hist):
                    ins = None
                    for i in range(4):
                        ins = e.transpose(out=ps[bank][:hist, i * 128:(i + 1) * 128],
                                          in_=buf[:, i, c0:c0 + hist], identity=ident)
                    return ins
                ADD("pe", _tr, reads=[(key, i) for i in range(4)] + [("vecs",)], writes=[("ps", bank)])
                ADD("act", lambda e, bank=bank, r=r, hist=hist: e.activation(
                    out=yt[:hist, r, 0:512], in_=ps[bank][:hist, :], func=AF.Copy),
                    reads=[("ps", bank)], writes=[("yt", r, 0)])
                ADD("sp", lambda e, r=r, hist=hist, dst=dst: e.dma_start(out=dst, in_=yt[:hist, r, 0:512]),
                      reads=[("yt", r, 0)], dma=("st", r))

        def store_out(cfg, halves=(0, 1)):
            xfm = xfm2[cfg.par]
            XK = lambda ft_: ("xfm", cfg.par, ft_)
            for half in halves:
                for (bi, dst) in cfg.outblocks:
                    c0, rows = cfg.blocks[bi]
                    r = bi % 2
                    bank = 4 + (bi % 2)

                    def _tr(e, half=half, bank=bank, c0=c0, rows=rows):
                        ins = None
                        for q in range(4):
                            ft = half * 4 + q
                            ins = e.transpose(out=ps[bank][:rows, q * 128:(q + 1) * 128],
                                              in_=xfm[:, ft, c0:c0 + rows], identity=ident)
                        return ins
                    ADD("pe", _tr, reads=[XK(half * 4 + q) for q in range(4)] + [("vecs",)],
                          writes=[("ps", bank)])
                    ADD("act", lambda e, half=half, bank=bank, rows=rows, r=r: e.activation(
                        out=yt[:rows, r, half * 512:(half + 1) * 512], in_=ps[bank][:rows, :], func=AF.Copy),
                        reads=[("ps", bank)], writes=[("yt", r, half)])
                    ADD("sp", lambda e, rows=rows, r=r, dst=dst, half=half: e.dma_start(
                        out=dst[:, half * 512:(half + 1) * 512], in_=yt[:rows, r, half * 512:(half + 1) * 512]),
                        reads=[("yt", r, half)], dma=("yout", r, half))
                    yield 1.0

        tiles = []
        tidx = 0
        for n in range(n_ptiles):
            kind = "P0" if n == 0 else "P"
            xsrc = xp_d[n * 512:(n + 1) * 512, :]
            outb = [(bi, yp_d[n * 512 + bi * 128:n * 512 + (bi + 1) * 128, :]) for bi in range(4)]
            tiles.append(TileCfg(f"P{n}", 512, 1, 512, [0], [(bi * 128, 128) for bi in range(4)], xsrc,
                                 outb, kind, tidx))
            tidx += 1
        if do_s:
            tiles.append(TileCfg("S", 192, 3, 64, [1, 2, 0], [(0, 128), (128, 64)], xs_d,
                                 [(0, ys_d[0:128, :]), (1, ys_d[128:192, :])], "S", tidx))
            tidx += 1
        N = len(tiles)
        for t_ in tiles:
            t_.par = t_.tidx % 2
        f_order.append((0, 0))
        for n in range(N):
            if n >= 1:
                f_order.append((n - 1, 2))
            if n + 1 < N:
                f_order.append((n + 1, 0))
        f_order.append((N - 1, 2))
        for i_, tk in enumerate(f_order):
            f_index[tk] = i_
        wst["ntiles"] = N

        def chain(*gs):
            for g in gs:
                yield from g

        marks = set([("f2in", -1)])

        def run(g):
            for c in g:
                if isinstance(c, tuple) and c[0] == "mark" and not dry[0]:
                    marks.add(c[1])

        def total(mk):
            dry[0] = True
            t = sum(c for c in mk() if not isinstance(c, tuple))
            dry[0] = False
            return max(t, 1e-6)

        def merge(mka, mkb, bbias=1.0):
            Ta, Tb = total(mka), total(mkb)
            gens = {"a": mka(), "b": mkb()}
            t = {"a": 0.0, "b": 0.0}
            T_ = {"a": Ta, "b": Tb * bbias}
            done = {"a": False, "b": False}
            need = {"a": None, "b": None}
            wantlock = {"a": False, "b": False}
            holder = [None]

            def blocked(sd):
                if need[sd] is not None and need[sd] not in marks:
                    return True
                if wantlock[sd]:
                    if holder[0] is None:
                        holder[0] = sd
                        wantlock[sd] = False
                        return False
                    return holder[0] != sd
                return False

            while not (done["a"] and done["b"]):
                cands = [sd for sd in ("a", "b") if not done[sd] and not blocked(sd)]
                if not cands:
                    raise RuntimeError("merge deadlock")
                if len(cands) == 2 and holder[0] == "a":
                    sd = "a"
                elif len(cands) == 2:
                    sd = "a" if t["a"] / T_["a"] <= t["b"] / T_["b"] else "b"
                else:
                    sd = cands[0]
                need[sd] = None
                try:
                    c = next(gens[sd])
                except StopIteration:
                    done[sd] = True
                    if holder[0] == sd:
                        holder[0] = None
                    continue
                if isinstance(c, tuple):
                    if c[0] == "mark":
                        marks.add(c[1])
                    elif c[0] == "need":
                        need[sd] = c[1]
                    elif c[0] == "lock":
                        if holder[0] is None or holder[0] == sd:
                            holder[0] = sd
                        else:
                            wantlock[sd] = True
                    elif c[0] == "unlock":
                        if holder[0] == sd:
                            holder[0] = None
                else:
                    t[sd] += c

        run(g_ada(0, 12))
        t0 = tiles[0]
        merge(lambda: chain(load_in(t0), ffn(t0, 0)), lambda: g_ada(12, 24))
        def tail(t_):
            return chain(ffn(t_, 2), layer_norm(t_, 2, hook=store_out(t_, halves=(0,))), store_out(t_, halves=(1,)))

        def head(t_):
            return chain(load_in(t_), ffn(t_, 0))

        for n in range(N):
            tn = tiles[n]
            mkB = lambda tn=tn: chain(layer_norm(tn, 0), mixer(tn), layer_norm(tn, 1))
            parts = []
            if n >= 1:
                parts.append(lambda n=n: tail(tiles[n - 1]))
            if n == 0:
                parts.append(lambda: g_ada(24, 36))
            if n + 1 < N:
                parts.append(lambda n=n: head(tiles[n + 1]))
            mkA = lambda parts=parts: chain(*[p() for p in parts])
            if parts and interleave:
                merge(mkA, mkB, bbias=1.04)
            else:
                run(mkA())
                run(mkB())
        run(tail(tiles[N - 1]))

        S.finalize()
        out_keys = [k for k in S.dma_counts if isinstance(k, tuple) and k[0] in ("yout", "st")]
        sem_es = contextlib.ExitStack()
        with sem_es:
            eng_sem = {e: sem_es.enter_context(nc.semaphore(f"sem_{e}")) for e in Sched.ENG}
            dma_sem = {}
            for i, key in enumerate(S.dma_counts):
                dma_sem[key] = sem_es.enter_context(nc.semaphore(f"dsem_{i}"))
            with nc.Block() as block:
                @block.sync
                def _(e):
                    S.emit_engine("sp", e, eng_sem, dma_sem, final_waits=out_keys)

                @block.tensor
                def _(e):
                    S.emit_engine("pe", e, eng_sem, dma_sem)

                @block.scalar
                def _(e):
                    S.emit_engine("act", e, eng_sem, dma_sem)

                @block.vector
                def _(e):
                    S.emit_engine("dve", e, eng_sem, dma_sem)

                @block.gpsimd
                def _(e):
                    S.emit_engine("pool", e, eng_sem, dma_sem)
    return nc, S


def _vec_tiles(v, nt):
    return np.ascontiguousarray(np.asarray(v, np.float32).reshape(nt, 128).T)


def prep_weights(inp):
    f32 = np.float32
    w_in = np.asarray(inp["ffn_w_in"], f32)[0]
    w1 = np.empty((2, 11, 128, 4096), f32)
    ct_order = []
    for c in range(11):
        for mt in range(4):
            j = 2 * c + mt // 2
            ct_order.append(j if mt % 2 == 0 else 22 + j)
    for f in range(2):
        W = w_in[f].reshape(8, 128, 44, 128)[:, :, ct_order, :]
        W = W.reshape(8, 128, 11, 4, 128).transpose(2, 1, 0, 3, 4)
        w1[f] = W.reshape(11, 128, 4096)
    w_dn = np.asarray(inp["ffn_w_down"], f32)[0]
    wd = np.empty((2, 8, 128, DFF), f32)
    for f in range(2):
        W = w_dn[f].reshape(22, 128, 8, 128).transpose(2, 1, 0, 3)
        wd[f] = W.reshape(8, 128, DFF)
    wmi_src = np.asarray(inp["mix_w_in"], f32)[0]
    W = wmi_src.reshape(8, 128, 12, 128)[:, :, MIX_CT, :]
    W = W.reshape(8, 128, 3, 4, 128).transpose(2, 1, 0, 3, 4)
    wmi = np.ascontiguousarray(W.reshape(3, 128, 4096))
    wpo = np.ascontiguousarray(np.asarray(inp["pool_w"], f32)[0].transpose(1, 0, 2).reshape(128, 512))
    wo_src = np.asarray(inp["mix_w_out"], f32)[0]
    W = wo_src.reshape(8, 128, 8, 128).transpose(2, 1, 0, 3)
    W = W.reshape(2, 4, 128, 8, 128).transpose(0, 2, 1, 3, 4)
    wo = np.ascontiguousarray(W.reshape(2, 128, 4096))
    ada_src = np.asarray(inp["ada_w"], f32)[0]
    ada = np.ascontiguousarray(ada_src.reshape(8, 128, 36, 256).transpose(2, 1, 0, 3).reshape(36, 128, 2048))
    return dict(w1=w1, wd=wd, wmi=wmi, wpo=wpo, wo=wo, ada_w=ada)


def prep_core(inp, i, shared_cols, npt=8):
    f32 = np.float32
    b, q = i // 4, i % 4
    xpr = np.asarray(inp["x_prompt"], f32)
    xsm = np.asarray(inp["x_sample"], f32)
    nmain = 512 * npt - 64
    if q > 0:
        halo = xpr[b, q * PCH - 64:q * PCH]
    else:
        halo = np.zeros((64, D), f32)
    xp = np.zeros((PCH, D), f32)
    xp[0:64] = halo
    xp[64:64 + nmain] = xpr[b, q * PCH:q * PCH + nmain]
    xs = np.ascontiguousarray(np.concatenate(
        [xsm[2 * i], xsm[2 * i + 1], xpr[b, q * PCH + nmain:q * PCH + nmain + 64]], axis=0))
    c3 = np.stack([np.asarray(inp["c_prompt"], f32)[b], np.asarray(inp["c_sample"], f32)[2 * i],
                   np.asarray(inp["c_sample"], f32)[2 * i + 1], np.asarray(inp["c_prompt"], f32)[b]], axis=0)
    cT = np.ascontiguousarray(c3.T.reshape(8, 128, NSEQ).transpose(1, 0, 2).reshape(128, 8 * NSEQ))
    cc = np.asarray(inp["cache_conv"], f32)[0, 2 * i:2 * i + 2]
    ccache = np.ascontiguousarray(cc.reshape(2, CH, 4, 128).transpose(3, 2, 0, 1).reshape(128, 4 * 2 * CH))
    pcs = np.asarray(inp["cache_pool"], f32)[0, 2 * i:2 * i + 2]
    pcache = np.ascontiguousarray(pcs.reshape(2, PH, 4, 128).transpose(3, 2, 0, 1).reshape(128, 4 * 2 * PH))
    vecs = shared_cols.copy()
    vecs[:, _off["mask"]] = 0.0 if q == 0 else 1.0
    inv = np.empty((4, 16), f32)
    for g in range(4):
        w = 2 ** (g + 1)
        for t in range(16):
            inv[g, t] = 1.0 / (min(t + 1, w) if q == 0 else w)
    vecs[:, _off["invcnt"]:_off["invcnt"] + 64] = inv.reshape(1, 64)
    return dict(xs=xs, xp=xp, cT=cT, ccache=ccache, pcache=pcache, vecs=vecs)


def shared_vecs(inp):
    f32 = np.float32
    v = np.zeros((128, NV), f32)
    v[:, _off["ident"]:_off["ident"] + 128] = np.eye(128, dtype=f32)
    v[:, _off["ada_b"]:_off["ada_b"] + 72] = _vec_tiles(np.asarray(inp["ada_b"])[0], 72)
    v[:, _off["ln_g"]:_off["ln_g"] + 24] = _vec_tiles(np.asarray(inp["ln_g"])[0].reshape(-1), 24)
    v[:, _off["ln_b"]:_off["ln_b"] + 24] = _vec_tiles(np.asarray(inp["ln_b"])[0].reshape(-1), 24)
    mb = _vec_tiles(np.asarray(inp["mix_b_in"])[0], 12)[:, MIX_CT]
    v[:, _off["mixb"]:_off["mixb"] + 12] = mb
    cw = np.asarray(inp["conv_w"], f32)[0]
    v[:, _off["conv_w"]:_off["conv_w"] + 124] = cw.reshape(31, 4, 128).transpose(2, 0, 1).reshape(128, 124)
    v[:, _off["conv_b"]:_off["conv_b"] + 4] = _vec_tiles(np.asarray(inp["conv_b"])[0], 4)
    v[:, _off["cln_g"]:_off["cln_g"] + 4] = _vec_tiles(np.asarray(inp["conv_ln_g"])[0], 4)
    v[:, _off["cln_b"]:_off["cln_b"] + 4] = _vec_tiles(np.asarray(inp["conv_ln_b"])[0], 4)
    v[:, _off["pool_b"]:_off["pool_b"] + 4] = _vec_tiles(np.asarray(inp["pool_b"])[0].reshape(-1), 4)
    v[:, _off["pool_s"]:_off["pool_s"] + 4] = _vec_tiles(np.asarray(inp["pool_scale"])[0], 4)
    v[:, _off["b_out"]:_off["b_out"] + 8] = _vec_tiles(np.asarray(inp["mix_b_out"])[0], 8)
    v[:, _off["eps"]] = EPS / (ALPHA * ALPHA)
    v[:, _off["eps"] + 1] = EPS
    return v


_NC_CACHE = {}


def kernel(**inputs):
    if "nc" not in _NC_CACHE:
        _NC_CACHE["nc"] = build_program()[0]
    nc = _NC_CACHE["nc"]
    wts = prep_weights(inputs)
    sv = shared_vecs(inputs)
    in_maps = []
    for i in range(NCORES):
        m = prep_core(inputs, i, sv)
        m.update(wts)
        in_maps.append(m)
    res = run_bass_kernel_spmd(nc, in_maps, core_ids=list(range(NCORES)))
    r = res.results
    f32 = np.float32
    y_prompt = np.empty((2, 16384, D), f32)
    y_sample = np.empty((16, 64, D), f32)
    scp = np.empty((1, 2, CH, 512), f32)
    spp = np.empty((1, 2, PH, 512), f32)
    scs = np.empty((1, 16, CH, 512), f32)
    sps = np.empty((1, 16, PH, 512), f32)
    for i in range(NCORES):
        b, q = i // 4, i % 4
        y_prompt[b, q * PCH:(q + 1) * PCH - 64] = r[i]["yp"][64:PCH]
        y_prompt[b, (q + 1) * PCH - 64:(q + 1) * PCH] = r[i]["ys"][128:192]
        y_sample[2 * i] = r[i]["ys"][0:64]
        y_sample[2 * i + 1] = r[i]["ys"][64:128]
        scs[0, 2 * i:2 * i + 2] = r[i]["scs"]
        sps[0, 2 * i:2 * i + 2] = r[i]["sps"]
        if q == 3:
            scp[0, b] = r[i]["scp"]
            spp[0, b] = r[i]["spp"]
    return (y_prompt, y_sample, scp, spp, scs, sps)
```
